# Optimizing a Trainium2 kernel written in Bass

```python
import jax, jax.numpy as jnp
from jax import lax
import numpy as np

D_MODEL = 2048
BATCH = 8
SEQ = 2048
DEPTH = 2

N_META = 16
RET_HEADS = 8
RET_DK = 128
RET_DV = D_MODEL // RET_HEADS
RET_CHUNK = 128
HG_HEADS = 8
HG_DK = 128
HG_DV = D_MODEL // HG_HEADS
HG_CHUNK = 16
D_FF = -(-8 * D_MODEL // (3 * 256)) * 256
RMS_EPS = 1e-6
ROPE_BASE = 10000.0
PAD = RET_CHUNK - N_META

RET_QK = RET_HEADS * RET_DK
RET_V = RET_HEADS * RET_DV
HG_QK = HG_HEADS * HG_DK
HG_V = HG_HEADS * HG_DV
IN_WIDTHS = (RET_QK, RET_QK, RET_V, RET_V, HG_QK, HG_QK, HG_V, HG_V, D_MODEL, D_MODEL)
IN_COLS = sum(IN_WIDTHS)
SPLIT_POINTS = tuple(sum(IN_WIDTHS[:i + 1]) for i in range(len(IN_WIDTHS) - 1))

kernel_name = "hybrid_retention_hgrn2_gated_block"


def rms_norm(x, w):
    xf = x.astype(jnp.float32)
    y = xf * lax.rsqrt(jnp.mean(xf * xf, axis=-1, keepdims=True) + RMS_EPS)
    return (y * w.astype(jnp.float32)).astype(x.dtype)


def group_rms_norm(x):
    xf = x.astype(jnp.float32)
    y = xf * lax.rsqrt(jnp.mean(xf * xf, axis=-1, keepdims=True) + RMS_EPS)
    return y.astype(x.dtype)


def rotary(x, pos):
    half = x.shape[-1] // 2
    inv = ROPE_BASE ** (-jnp.arange(half, dtype=jnp.float32) / half)
    ang = pos.astype(jnp.float32)[:, None] * inv[None, :]
    cos = jnp.cos(ang)[None, :, None, :]
    sin = jnp.sin(ang)[None, :, None, :]
    xf = x.astype(jnp.float32)
    x1, x2 = xf[..., :half], xf[..., half:]
    return jnp.concatenate([x1 * cos - x2 * sin, x1 * sin + x2 * cos], axis=-1).astype(x.dtype)


def retention(q, k, v):
    B, L = q.shape[0], q.shape[1]
    C = RET_CHUNK
    N = L // C
    dt = q.dtype
    log_g = jnp.log1p(-jnp.exp2(-5.0 - jnp.arange(RET_HEADS, dtype=jnp.float32)))
    qc = q.reshape(B, N, C, RET_HEADS, RET_DK)
    kc = k.reshape(B, N, C, RET_HEADS, RET_DK)
    vc = v.reshape(B, N, C, RET_HEADS, RET_DV)
    idx = jnp.arange(C, dtype=jnp.float32)
    diff = idx[:, None] - idx[None, :]
    decay_intra = jnp.where(diff[None] >= 0,
                            jnp.exp(jnp.maximum(diff, 0.0)[None] * log_g[:, None, None]), 0.0)
    scores = jnp.einsum('bnchd,bnmhd->bnhcm', qc, kc) * decay_intra.astype(dt)[None, None]
    y_intra = jnp.einsum('bnhcm,bnmhe->bnche', scores, vc)
    k_dec = jnp.exp((C - 1 - idx)[:, None] * log_g[None, :]).astype(dt)
    delta = jnp.einsum('bnchd,ch,bnche->nbhde', kc, k_dec, vc)
    chunk_decay = jnp.exp(C * log_g).astype(delta.dtype)[None, :, None, None]

    def step(S, d):
        return S * chunk_decay + d, S

    _, S_prev = lax.scan(step, jnp.zeros(delta.shape[1:], delta.dtype), delta)
    q_dec = jnp.exp((idx + 1)[:, None] * log_g[None, :]).astype(dt)
    y_inter = jnp.einsum('bnchd,ch,nbhde->bnche', qc, q_dec, S_prev)
    return (y_intra + y_inter).reshape(B, L, RET_HEADS, RET_DV)


def hgrn2(q, log_f, k, v):
    B, L = q.shape[0], q.shape[1]
    C = HG_CHUNK
    N = L // C

    def chunks(a):
        return jnp.moveaxis(a.reshape((B, N, C) + a.shape[2:]), 1, 0)

    qc, kc, vc = chunks(q), chunks(k), chunks(v)
    bc_all = lax.cumsum(chunks(log_f), axis=2)
    mask = (jnp.arange(C)[:, None] >= jnp.arange(C)[None, :])[None, :, :, None, None]

    def step(S, xs):
        qt, kt, vt, bt = xs
        dt = qt.dtype
        rel = bt[:, :, None] - bt[:, None, :]
        decay = jnp.where(mask, jnp.exp(jnp.where(mask, rel, 0.0)), 0.0).astype(dt)
        A = jnp.einsum('btshd,bthd,bshd->bhts', decay, qt, kt)
        y_intra = jnp.einsum('bhts,bshe->bthe', A, vt)
        y_inter = jnp.einsum('bthd,bhde->bthe', qt * jnp.exp(bt).astype(dt), S)
        b_last = bt[:, -1]
        k_end = kt * jnp.exp(b_last[:, None] - bt).astype(dt)
        S_new = S * jnp.exp(b_last)[..., None].astype(S.dtype) + jnp.einsum('bshd,bshe->bhde', k_end, vt)
        return S_new, y_intra + y_inter

    S0 = jnp.zeros((B, HG_HEADS, HG_DK, HG_DV), v.dtype)
    _, ys = lax.scan(step, S0, (qc, kc, vc, bc_all))
    return jnp.moveaxis(ys, 0, 1).reshape(B, L, HG_HEADS, HG_DV)


def mixer(h, pos, valid, lb, w_in, hg_norm_w, w_br_ret, w_br_hg, w_out):
    B, L, _ = h.shape
    dt = h.dtype
    proj = h @ w_in
    rq, rk, rv, rg, hq, hf, hi, hg, gate_ret, gate_hg = jnp.split(proj, SPLIT_POINTS, axis=-1)
    vmask = valid[None, :, None, None]

    rq = rotary(rq.reshape(B, L, RET_HEADS, RET_DK), pos)
    rk = rotary(rk.reshape(B, L, RET_HEADS, RET_DK), pos) * (RET_DK ** -0.5)
    rk = jnp.where(vmask, rk, jnp.zeros_like(rk))
    yr = retention(rq, rk, rv.reshape(B, L, RET_HEADS, RET_DV))
    yr = group_rms_norm(yr).reshape(B, L, RET_V) * jax.nn.silu(rg)
    yr = yr @ w_br_ret

    z = hf.astype(jnp.float32).reshape(B, L, HG_HEADS, HG_DK)
    lbh = lb.reshape(HG_HEADS, HG_DK)
    one_minus_f = (1.0 - lbh) * jax.nn.sigmoid(-z)
    log_f = jnp.log1p(-one_minus_f)
    log_f = jnp.where(vmask, log_f, 0.0)
    kin = jnp.where(vmask, one_minus_f, 0.0).astype(dt)
    qh = jax.nn.silu(hq).reshape(B, L, HG_HEADS, HG_DK)
    yh = hgrn2(qh, log_f, kin, hi.reshape(B, L, HG_HEADS, HG_DV))
    yh = rms_norm(yh, hg_norm_w).reshape(B, L, HG_V) * jax.nn.silu(hg)
    yh = yh @ w_br_hg

    y = jax.nn.sigmoid(gate_ret) * yr + jax.nn.sigmoid(gate_hg) * yh
    return y @ w_out


def swiglu(h, w_gate, w_up, w_down):
    return (jax.nn.silu(h @ w_gate) * (h @ w_up)) @ w_down


def setup_inputs(seed: int = 0) -> dict:
    key = jax.random.key(seed)
    ks = jax.random.split(key, 16)
    f32 = jnp.float32

    def normal(k, shape, scale):
        return jax.random.normal(k, shape, f32) * scale

    def gains(k, shape):
        return 1.0 + 0.1 * jax.random.normal(k, shape, f32)

    return {
        "x": normal(ks[0], (BATCH, SEQ, D_MODEL), 1.0),
        "meta_tokens": normal(ks[1], (N_META, D_MODEL), 1.0),
        "norm_mix_pre": gains(ks[2], (DEPTH, D_MODEL)),
        "norm_mix_post": gains(ks[3], (DEPTH, D_MODEL)),
        "norm_ffn_pre": gains(ks[4], (DEPTH, D_MODEL)),
        "norm_ffn_post": gains(ks[5], (DEPTH, D_MODEL)),
        "w_in": normal(ks[6], (DEPTH, D_MODEL, IN_COLS), D_MODEL ** -0.5),
        "hg_lb_logits": normal(ks[7], (DEPTH, HG_QK), 0.1),
        "hg_norm_w": gains(ks[8], (DEPTH, HG_DV)),
        "w_br_ret": normal(ks[9], (DEPTH, RET_V, D_MODEL), RET_V ** -0.5),
        "w_br_hg": normal(ks[10], (DEPTH, HG_V, D_MODEL), HG_V ** -0.5),
        "w_out": normal(ks[11], (DEPTH, D_MODEL, D_MODEL), D_MODEL ** -0.5),
        "w_ffn_gate": normal(ks[12], (DEPTH, D_MODEL, D_FF), D_MODEL ** -0.5),
        "w_ffn_up": normal(ks[13], (DEPTH, D_MODEL, D_FF), D_MODEL ** -0.5),
        "w_ffn_down": normal(ks[14], (DEPTH, D_FF, D_MODEL), D_FF ** -0.5),
    }


def reference(x, meta_tokens, norm_mix_pre, norm_mix_post, norm_ffn_pre, norm_ffn_post, w_in, hg_lb_logits,
              hg_norm_w, w_br_ret, w_br_hg, w_out, w_ffn_gate, w_ffn_up, w_ffn_down):
    B = x.shape[0]
    meta = jnp.broadcast_to(meta_tokens.astype(x.dtype)[None], (B, N_META, D_MODEL))
    pad = jnp.zeros((B, PAD, D_MODEL), x.dtype)
    h = jnp.concatenate([pad, meta, x], axis=1)
    L = h.shape[1]
    pos = jnp.arange(L, dtype=jnp.int32) - PAD
    valid = pos >= 0

    lb_sm = jax.nn.softmax(hg_lb_logits.astype(jnp.float32), axis=0)
    lbs = jnp.cumsum(lb_sm, axis=0) - lb_sm[0:1]

    for l in range(DEPTH):
        m = mixer(rms_norm(h, norm_mix_pre[l]), pos, valid, lbs[l], w_in[l], hg_norm_w[l],
                  w_br_ret[l], w_br_hg[l], w_out[l])
        h = h + rms_norm(m, norm_mix_post[l])
        f = swiglu(rms_norm(h, norm_ffn_pre[l]), w_ffn_gate[l], w_ffn_up[l], w_ffn_down[l])
        h = h + rms_norm(f, norm_ffn_post[l])
    return h[:, PAD + N_META:]
```

```python
import math
import numpy as np
import concourse.bass as bass
import concourse.mybir as mybir
from concourse.bass_utils import run_bass_kernel_spmd

F32 = mybir.dt.float32
BF16 = mybir.dt.bfloat16
AF = mybir.ActivationFunctionType
ALU = mybir.AluOpType
AX = mybir.AxisListType

D = 2048
T = 2064
NMETA = 16
DFF = 5632
KCF = DFF // 128
DEPTH = 2
EPS = 1e-6
INC = 16384
LOGG = [math.log1p(-2.0 ** (-5 - h)) for h in range(8)]

ENGS = ["pe", "act", "dve", "pool", "sp"]
EPOCH = 16000
SAME_ENGINE_SYNC = {"pe": False, "act": True, "dve": True, "pool": True, "sp": False}


class _Rec:
    def __init__(self):
        self.call = None

    def __getattr__(self, name):
        def f(*a, **k):
            self.call = (name, a, k)
            return None
        return f


class Sched:
    def __init__(self, nc):
        self.nc = nc
        self.q = {e: [] for e in ENGS}
        self.cnt = {e: 0 for e in ENGS}
        self.seen = {e: {} for e in ENGS}
        self.last_w = {}
        self.readers = {}
        self.sem_handles = {}
        self.dma_cnt = {}

    def sem(self, key):
        h = self.sem_handles.get(key)
        if h is None:
            name = "s" + str(len(self.sem_handles))
            h = self.nc.alloc_semaphore(name)
            self.sem_handles[key] = h
        return h

    def _deps(self, eng, reads, writes, force_same=False):
        deps = []
        for b in reads:
            t = self.last_w.get(b)
            if t is not None:
                deps.append(t)
        for b in writes:
            t = self.last_w.get(b)
            if t is not None:
                deps.append(t)
            deps.extend(self.readers.get(b, ()))
        waits = {}
        seen = self.seen[eng]
        for (k, v) in deps:
            if k[0] == eng and not (SAME_ENGINE_SYNC[eng] or force_same):
                continue
            if seen.get(k, 0) >= v:
                continue
            if waits.get(k, 0) < v:
                waits[k] = v
        for k, v in waits.items():
            seen[k] = v
        return list(waits.items())

    def _commit(self, tok, reads, writes):
        for b in writes:
            self.last_w[b] = tok
            self.readers[b] = []
        for b in reads:
            self.readers.setdefault(b, []).append(tok)

    def op(self, eng, fn, reads=(), writes=()):
        rec = _Rec()
        fn(rec)
        fn = rec.call
        waits = self._deps(eng, reads, writes)
        c = self.cnt[eng]
        key = (eng, c // EPOCH)
        val = c % EPOCH + 1
        self.cnt[eng] = c + 1
        self.sem(key)
        self.q[eng].append((waits, fn, key, 1))
        self._commit((key, val), reads, writes)

    def dma(self, eng, out, in_, semkey, reads=(), writes=(), **kw):
        waits = self._deps(eng, reads, writes, force_same=True)
        key = ("d", semkey)
        n = self.dma_cnt.get(key, 0) + 1
        self.dma_cnt[key] = n
        self.sem(key)
        self.q[eng].append((waits, lambda e: e.dma_start(out=out, in_=in_, **kw), key, 16))
        self._commit((key, 16 * n), reads, writes)

    def drain(self):
        best = {}
        for t in list(self.last_w.values()) + [x for ts in self.readers.values() for x in ts]:
            k, v = t
            if best.get(k, 0) < v:
                best[k] = v
        for e in ENGS:
            waits = []
            for k, v in best.items():
                if self.seen[e].get(k, 0) >= v:
                    continue
                self.seen[e][k] = v
                waits.append((k, v))
            if waits:
                self.q[e].append((waits, None, None, 0))

    def forget(self, names):
        for n in names:
            self.last_w.pop(n, None)
            self.readers.pop(n, None)

    def replay(self, block):
        handles = self.sem_handles

        def run(engname):
            def body(e):
                for waits, fn, key, inc in self.q[engname]:
                    for k, v in waits:
                        e.wait_ge(handles[k], v)
                    if fn is not None:
                        if isinstance(fn, tuple):
                            ins = getattr(e, fn[0])(*fn[1], **fn[2])
                        else:
                            ins = fn(e)
                        ins.then_inc(handles[key], inc)
            return body

        block.tensor(run("pe"))
        block.scalar(run("act"))
        block.vector(run("dve"))
        block.gpsimd(run("pool"))
        block.sync(run("sp"))


def tok_tiles(t0, n, step=344):
    out = []
    a = t0
    while a < t0 + n:
        m = min(step, t0 + n - a)
        out.append((a, m))
        a += m
    return out


BLOCKS = [(0, 16)] + [(16 + 128 * i, 128) for i in range(16)]
SUBS = [(0, 16)] + [(16 + 64 * i, 64) for i in range(32)]
HALVES = [(0, 1032), (1032, 1032)]


def build_nc(debug=False, stop_after=None):
    nc = bass.Bass("TRN2", target_bir_lowering=False)
    S = Sched(nc)
    okind = "ExternalOutput" if debug else "Internal"

    def dram_in(name, shape):
        return nc.dram_tensor(name, shape, F32, kind="ExternalInput").ap()

    def scratch(name, shape, dt):
        if debug:
            return nc.dram_tensor(name, shape, dt, kind="ExternalOutput").ap()
        return nc.dram_tensor(name, shape, dt).ap()

    x = dram_in("x", [2048, D])
    meta = dram_in("meta_tokens", [NMETA, D])
    g_mix_pre = dram_in("norm_mix_pre", [DEPTH, D])
    g_mix_post = dram_in("norm_mix_post", [DEPTH, D])
    g_ffn_pre = dram_in("norm_ffn_pre", [DEPTH, D])
    g_ffn_post = dram_in("norm_ffn_post", [DEPTH, D])
    w_in = dram_in("w_in", [DEPTH, D, INC])
    lb_logits = dram_in("hg_lb_logits", [DEPTH, 1024])
    hg_norm_w = dram_in("hg_norm_w", [DEPTH, 256])
    w_br_ret = dram_in("w_br_ret", [DEPTH, D, D])
    w_br_hg = dram_in("w_br_hg", [DEPTH, D, D])
    w_out = dram_in("w_out", [DEPTH, D, D])
    w_gate = dram_in("w_ffn_gate", [DEPTH, D, DFF])
    w_up = dram_in("w_ffn_up", [DEPTH, D, DFF])
    w_down = dram_in("w_ffn_down", [DEPTH, DFF, D])
    out = nc.dram_tensor("out", [2048, D], F32, kind="ExternalOutput").ap()

    hT = scratch("hT", [D, T], F32)
    mT = scratch("mT", [D, T], F32)
    GT = scratch("GT", [2 * D, T], BF16)
    yT = {"r": scratch("yrT", [D, T], BF16), "h": scratch("yhT", [D, T], BF16)}
    Qs = {"r": scratch("Qr", [T, 1024], BF16), "h": scratch("Qh", [T, 1024], BF16)}
    Ks = {"r": scratch("Kr", [T, 1024], BF16), "h": scratch("Kh", [T, 1024], BF16)}
    Vs = {"r": scratch("Vr", [T, 2048], BF16), "h": scratch("Vh", [T, 2048], BF16)}
    Gs = {"r": scratch("Gr", [T, 2048], BF16), "h": scratch("Gh", [T, 2048], BF16)}
    LFh = scratch("LFh", [T, 1024], F32)

    def fm(ap):
        return ap.rearrange("(kc p) t -> p kc t", p=128)

    hT_v, mT_v, GT_v = fm(hT), fm(mT), fm(GT)
    yT_v = {k: fm(v) for k, v in yT.items()}

    from contextlib import ExitStack
    es_all = ExitStack()

    uniq = [0]

    def sb(es, name, shape, dt):
        uniq[0] += 1
        return es.enter_context(nc.sbuf_tensor("%s_u%d" % (name, uniq[0]), shape, dt))

    PS = [es_all.enter_context(nc.psum_tensor("psb%d" % i, [128, 512], F32)) for i in range(8)]

    def psf(i):
        return PS[i]

    def psb(i):
        return PS[i][:].bitcast(BF16)

    cs = es_all
    ident_b = sb(cs, "ident_b", [128, 128], BF16)
    ident_f = sb(cs, "ident_f", [128, 128], F32)
    ones_b = sb(cs, "ones_b", [128, 128], BF16)
    mask64 = sb(cs, "mask64", [64, 64], F32)
    mm64 = sb(cs, "mm64", [64, 66], F32)
    mm16 = sb(cs, "mm16", [64, 18], F32)
    m2 = sb(cs, "m2", [64, 64], F32)
    lfr = sb(cs, "lfr", [64, 1024], F32)
    epsT = sb(cs, "epsT", [128, 1], F32)
    piT = sb(cs, "piT", [128, 1], F32)
    gvec = sb(cs, "gvec", [128, 4, DEPTH, 16], F32)
    oneT = sb(cs, "oneT", [128, 1], F32)
    rot_d = scratch("rot_d", [128, 4 * 17 * 128], F32)
    oml_d = scratch("oml_d", [128, DEPTH * 1024], F32)
    w8_d = scratch("w8_d", [128, DEPTH * 2048], F32)

    with ExitStack() as es:
        oml = sb(es, "oml", [128, DEPTH, 1024], F32)
        w8 = sb(es, "w8", [128, DEPTH, 2048], F32)
        rot = sb(es, "rot", [128, 4, 17, 128], F32)
        dmat = sb(es, "dmat", [128, 128], F32)
        pcol = sb(es, "pcol", [128, 128], F32)
        tmpa = sb(es, "tmpa", [128, 128], F32)
        tmpb = sb(es, "tmpb", [128, 128], F32)
        pos = sb(es, "pos", [128, 17], F32)
        jj = sb(es, "jj", [128, 64], F32)
        inv = sb(es, "inv", [128, 64], F32)
        ang = sb(es, "ang", [128, 17, 64], F32)
        ang2 = sb(es, "ang2", [128, 17, 64], F32)
        lbt = sb(es, "lbt", [128, DEPTH, 1024], F32)

        S.op("pool", lambda e: e.iota(dmat[:], pattern=[[1, 128]], base=0, channel_multiplier=-1,
                                      allow_small_or_imprecise_dtypes=True), writes=["dmat"])
        S.op("pool", lambda e: e.iota(pcol[:], pattern=[[0, 128]], base=0, channel_multiplier=1,
                                      allow_small_or_imprecise_dtypes=True), writes=["pcol"])
        S.op("pool", lambda e: e.iota(pos[:], pattern=[[128, 17]], base=-112, channel_multiplier=1,
                                      allow_small_or_imprecise_dtypes=True), writes=["pos"])
        S.op("pool", lambda e: e.iota(pos[:, 0:1], pattern=[[0, 1]], base=0, channel_multiplier=1,
                                      allow_small_or_imprecise_dtypes=True), reads=["pos"], writes=["pos"])
        S.op("pool", lambda e: e.iota(jj[:], pattern=[[1, 64]], base=0, channel_multiplier=0,
                                      allow_small_or_imprecise_dtypes=True), writes=["jj"])
        S.op("dve", lambda e: e.tensor_single_scalar(ident_f[:], dmat[:], 0.0, ALU.is_equal), reads=["dmat"], writes=["ident_f"])
        S.op("dve", lambda e: e.tensor_copy(ident_b[:], ident_f[:]), reads=["ident_f"], writes=["ident_b"])
        S.op("dve", lambda e: e.memset(ones_b[:], 1.0), writes=["ones_b"])
        S.op("dve", lambda e: e.memset(epsT[:], EPS), writes=["epsT"])
        S.op("dve", lambda e: e.memset(piT[:], math.pi), writes=["piT"])
        S.op("dve", lambda e: e.memset(oneT[:], 1.0), writes=["oneT"])
        S.op("dve", lambda e: e.tensor_single_scalar(mask64[:], dmat[0:64, 0:64], 0.0, ALU.is_ge), reads=["dmat"], writes=["mask64"])
        S.op("dve", lambda e: e.tensor_single_scalar(m2[:], dmat[0:64, 0:64], 0.0, ALU.is_lt), reads=["dmat"], writes=["m2"])
        for (mm, n, mid) in ((mm64, 64, 31), (mm16, 16, 7)):
            nm = "mm%d" % n
            S.op("dve", lambda e, mid=mid: e.tensor_single_scalar(tmpa[:], pcol[:], float(mid), ALU.is_le), reads=["pcol"], writes=["tmpa"])
            S.op("dve", lambda e: e.tensor_single_scalar(tmpb[:], dmat[:], 0.0, ALU.is_ge), reads=["dmat"], writes=["tmpb"])
            S.op("dve", lambda e, mm=mm, n=n: e.tensor_tensor(mm[0:64, 0:n], tmpb[0:64, 0:n], tmpa[0:64, 0:n], ALU.subtract),
                 reads=["tmpa", "tmpb"], writes=[nm])
            S.op("dve", lambda e, mm=mm, n=n: e.tensor_copy(mm[0:64, n:n + 1], tmpa[0:64, 0:1]), reads=["tmpa"], writes=[nm])
            S.op("dve", lambda e, mm=mm, n=n: e.memset(mm[0:64, n + 1:n + 2], 1.0), writes=[nm])
        for h in range(8):
            S.op("pool", lambda e, h=h: e.memset(lfr[:, h * 128:(h + 1) * 128], LOGG[h]), writes=["lfr"])
        for wi, g in enumerate((g_mix_pre, g_mix_post, g_ffn_pre, g_ffn_post)):
            for l in range(DEPTH):
                S.dma("sp", gvec[:, wi, l, :], g[l].rearrange("(kc p) -> p kc", p=128), "gvec", writes=["gvec"],
                      allow_slow_non_contiguous=True)
        for l in range(DEPTH):
            S.dma("sp", lbt[:, l, :], lb_logits[l:l + 1, :].broadcast_to([128, 1024]), "lbt", writes=["lbt"])
            for j in range(8):
                S.dma("sp", w8[:, l, j * 256:(j + 1) * 256], hg_norm_w[l:l + 1, :].broadcast_to([128, 256]), "w8", writes=["w8"])
        S.op("dve", lambda e: e.memset(oml[:, 0, :], 1.0), writes=["oml"])
        S.op("dve", lambda e: e.tensor_tensor(lbt[:, 0, :], lbt[:, 0, :], lbt[:, 1, :], ALU.subtract), reads=["lbt"], writes=["lbt"])
        S.op("act", lambda e: e.activation(out=oml[:, 1, :], in_=lbt[:, 0, :], func=AF.Sigmoid), reads=["lbt"], writes=["oml"])
        for j_ in range(64):
            S.op("pool", lambda e, j_=j_: e.memset(inv[:, j_:j_ + 1], float(np.float32(10000.0 ** (-j_ / 64.0)))), writes=["inv"])
        S.op("dve", lambda e: e.tensor_tensor(ang[:], pos[:].unsqueeze(2).broadcast_to([128, 17, 64]),
                                              inv[:].unsqueeze(1).broadcast_to([128, 17, 64]), ALU.mult),
             reads=["pos", "inv"], writes=["ang"])
        S.op("dve", lambda e: e.tensor_single_scalar(ang2[:], ang[:], math.pi / 2, ALU.add), reads=["ang"], writes=["ang2"])
        angi = sb(es, "angi", [128, 17, 64], mybir.dt.int32)
        angk = sb(es, "angk", [128, 17, 64], F32)
        for (a_, an_) in ((ang, "ang"), (ang2, "ang2")):
            S.op("dve", lambda e, a_=a_: e.tensor_single_scalar(angk[:], a_[:], 1.0 / (2 * math.pi), ALU.mult), reads=[an_], writes=["angk"])
            S.op("dve", lambda e: e.tensor_copy(angi[:], angk[:]), reads=["angk"], writes=["angi"])
            S.op("dve", lambda e: e.tensor_copy(angk[:], angi[:]), reads=["angi"], writes=["angk"])
            S.op("dve", lambda e, a_=a_: e.scalar_tensor_tensor(out=a_[:], in0=angk[:], scalar=-2 * math.pi, in1=a_[:], op0=ALU.mult, op1=ALU.add),
                 reads=["angk", an_], writes=[an_])
            S.op("dve", lambda e, a_=a_: e.tensor_single_scalar(angk[:], a_[:], math.pi, ALU.is_gt), reads=[an_], writes=["angk"])
            S.op("dve", lambda e, a_=a_: e.scalar_tensor_tensor(out=a_[:], in0=angk[:], scalar=-2 * math.pi, in1=a_[:], op0=ALU.mult, op1=ALU.add),
                 reads=["angk", an_], writes=[an_])
            S.op("dve", lambda e, a_=a_: e.tensor_single_scalar(angk[:], a_[:], -math.pi, ALU.is_lt), reads=[an_], writes=["angk"])
            S.op("dve", lambda e, a_=a_: e.scalar_tensor_tensor(out=a_[:], in0=angk[:], scalar=2 * math.pi, in1=a_[:], op0=ALU.mult, op1=ALU.add),
                 reads=["angk", an_], writes=[an_])
            S.op("dve", lambda e, a_=a_: e.tensor_scalar(a_[:], a_[:], -3.14159, 3.14159, ALU.max, ALU.min), reads=[an_], writes=[an_])
        for half in range(2):
            S.op("act", lambda e, half=half: e.activation(out=rot[:, 0, :, half * 64:(half + 1) * 64], in_=ang2[:], func=AF.Sin),
                 reads=["ang2"], writes=["rot"])
            S.op("act", lambda e, half=half: e.activation(out=rot[:, 1, :, half * 64:(half + 1) * 64], in_=ang[:], func=AF.Sin),
                 reads=["ang"], writes=["rot"])
        S.op("dve", lambda e: e.tensor_single_scalar(rot[:, 2:4].rearrange("p a b c -> p (a b c)"), rot[:, 0:2].rearrange("p a b c -> p (a b c)"),
                                                     128.0 ** -0.5, ALU.mult), reads=["rot"], writes=["rot"])
        S.dma("sp", rot_d, rot[:].rearrange("p a b c -> p (a b c)"), "rot", reads=["rot"], writes=["rot_d"])
        S.dma("sp", oml_d, oml[:].rearrange("p a b -> p (a b)"), "oml", reads=["oml"], writes=["oml_d"])
        S.dma("sp", w8_d, w8[:].rearrange("p a b -> p (a b)"), "w8", reads=["w8"], writes=["w8_d"])
        S.drain()
    S.forget(["dmat", "pcol", "tmpa", "tmpb", "pos", "jj", "inv", "ang", "ang2", "lbt"])

    pscount = [0]

    def wload(es_name, dst, src_view, key):
        S.dma("pool", dst, src_view, key, writes=[key])

    def norm_pass(l, toks, m_gain, pre_gain, XT, xt_name, xt_t0, tag):
        with ExitStack() as es:
            NT = 344
            ht = [sb(es, "np_ht%d" % i, [128, 16, NT], F32) for i in range(2)]
            mt = [sb(es, "np_mt%d" % i, [128, 16, NT], F32) for i in range(2)] if m_gain is not None else [None, None]
            sq = sb(es, "np_sq", [128, 16, NT], BF16)
            tmp = sb(es, "np_tmp", [128, 16, NT], F32)
            rs = sb(es, "np_rs", [128, NT], F32)
            tiles = []
            for (t0, n) in toks:
                tiles += tok_tiles(t0, n, NT)
            for i, (t0, n) in enumerate(tiles):
                j = i % 2
                hb, mb = "np_ht%d" % j, "np_mt%d" % j
                S.dma("sp", ht[j][:, :, 0:n], hT_v[:, :, t0:t0 + n], hb, reads=["hT"], writes=[hb])
                if m_gain is not None:
                    S.dma("sp", mt[j][:, :, 0:n], mT_v[:, :, t0:t0 + n], mb, reads=["mT"], writes=[mb])

                def rstd(src, srcname):
                    bk = pscount[0] % 2
                    pscount[0] += 1
                    pn = "ps%d" % bk
                    S.op("act", lambda e: e.activation(out=sq[:, :, 0:n], in_=src[:, :, 0:n], func=AF.Square), reads=[srcname], writes=["np_sq"])
                    for kc in range(16):
                        S.op("pe", lambda e, kc=kc: e.matmul(psf(bk)[:, 0:n], lhsT=ones_b[:], rhs=sq[:, kc, 0:n], start=(kc == 0), stop=(kc == 15)),
                             reads=["np_sq", "ones_b"], writes=[pn])
                    S.op("act", lambda e: e.activation(out=rs[:, 0:n], in_=psf(bk)[:, 0:n], func=AF.Sqrt, scale=1.0 / D, bias=epsT[:, 0:1]),
                         reads=[pn, "epsT"], writes=["np_rs"])
                    S.op("dve", lambda e: e.reciprocal(rs[:, 0:n], rs[:, 0:n]), reads=["np_rs"], writes=["np_rs"])

                def scale_by(dst, dstname, src, srcname, gidx, eng2="pool"):
                    S.op("dve", lambda e: e.tensor_tensor(tmp[:, :, 0:n], src[:, :, 0:n], rs[:, 0:n].unsqueeze(1).broadcast_to([128, 16, n]), ALU.mult),
                         reads=[srcname, "np_rs"], writes=["np_tmp"])
                    S.op(eng2, lambda e: e.tensor_tensor(dst, tmp[:, :, 0:n], gvec[:, gidx, l, :].unsqueeze(2).broadcast_to([128, 16, n]), ALU.mult),
                         reads=["np_tmp", "gvec"], writes=[dstname])

                if m_gain is not None:
                    rstd(mt[j], mb)
                    scale_by(mt[j][:, :, 0:n], mb, mt[j], mb, m_gain)
                    S.op("dve", lambda e: e.tensor_tensor(ht[j][:, :, 0:n], ht[j][:, :, 0:n], mt[j][:, :, 0:n], ALU.add), reads=[hb, mb], writes=[hb])
                    S.dma("sp", hT_v[:, :, t0:t0 + n], ht[j][:, :, 0:n], hb, reads=[hb], writes=["hT"])
                if pre_gain is not None:
                    rstd(ht[j], hb)
                    scale_by(XT[:, :, t0 - xt_t0:t0 - xt_t0 + n], xt_name, ht[j], hb, pre_gain)
            S.drain()
        S.forget(["np_ht0", "np_ht1", "np_mt0", "np_mt1", "np_sq", "np_tmp", "np_rs"])

    with ExitStack() as es:
        xin = [sb(es, "xin%d" % i, [128, D], F32) for i in range(2)]
        xo = [sb(es, "xo%d" % i, [128, 16, 128], F32) for i in range(2)]
        for b, (t0, n) in enumerate(BLOCKS):
            j = b % 2
            src = meta[:, :] if b == 0 else x[t0 - 16:t0 - 16 + n, :]
            S.dma("sp", xin[j][0:n, :], src, "xin%d" % j, writes=["xin%d" % j])
            for q4 in range(4):
                bk = pscount[0] % 2
                pscount[0] += 1
                for i4 in range(4):
                    kc = q4 * 4 + i4
                    S.op("pe", lambda e, kc=kc, i4=i4, bk=bk: e.transpose(psf(bk)[:, i4 * 128:i4 * 128 + n], xin[j][0:n, kc * 128:(kc + 1) * 128], ident_f[0:n, 0:n]),
                         reads=["xin%d" % j, "ident_f"], writes=["ps%d" % bk])
                S.op("act" if q4 % 2 else "dve",
                     (lambda e, bk=bk, q4=q4: e.activation(out=xo[j][:, q4 * 4:q4 * 4 + 4, 0:n], in_=psf(bk)[:].rearrange("p (a b) -> p a b", a=4)[:, :, 0:n], func=AF.Copy))
                     if q4 % 2 else
                     (lambda e, bk=bk, q4=q4: e.tensor_copy(xo[j][:, q4 * 4:q4 * 4 + 4, 0:n], psf(bk)[:].rearrange("p (a b) -> p a b", a=4)[:, :, 0:n])),
                     reads=["ps%d" % bk], writes=["xo%d" % j])
            S.dma("sp", hT_v[:, :, t0:t0 + n], xo[j][:, :, 0:n], "xo%d" % j, reads=["xo%d" % j], writes=["hT"])
        S.drain()
    S.forget(["xin0", "xin1", "xo0", "xo1"])

    def finish():
        S.drain()
        with nc.Block() as block:
            S.replay(block)
        es_all.close()
        return nc

    if stop_after == "init":
        return finish()

    for l in range(DEPTH):
        with ExitStack() as es:
            hnT = sb(es, "hnT", [128, 16, T], BF16)
            norm_pass(l, [(0, T)], None, 0, hnT, "hnT", 0, "p1")
            rot = sb(es, "rot2", [128, 4, 17, 128], F32)
            oml = sb(es, "oml2", [128, 1, 1024], F32)
            w8 = sb(es, "w82", [128, 1, 2048], F32)
            S.dma("sp", rot[:].rearrange("p a b c -> p (a b c)"), rot_d, "rot", reads=["rot_d"], writes=["rot"])
            S.dma("sp", oml[:, 0, :], oml_d[:, l * 1024:(l + 1) * 1024], "oml", reads=["oml_d"], writes=["oml"])
            S.dma("sp", w8[:, 0, :], w8_d[:, l * 2048:(l + 1) * 2048], "w8", reads=["w8_d"], writes=["w8"])
            wb = [sb(es, "wb%d" % i, [128, 16, 512], BF16) for i in range(2)]
            st = [sb(es, "st%d" % i, [128, 512], F32) for i in range(2)]
            st2 = [sb(es, "stb%d" % i, [128, 512], F32) for i in range(2)]
            ob = [sb(es, "ob%d" % i, [128, 512], BF16) for i in range(2)]
            of = [sb(es, "of%d" % i, [128, 512], F32) for i in range(2)]
            wv = w_in[l].rearrange("(kc p) n -> p kc n", p=128)
            NG = INC // 512
            S.dma("pool", wb[0][:], wv[:, :, 0:512], "wb0", writes=["wb0"])
            it = 0
            for g in range(NG):
                wj = g % 2
                if g + 1 < NG:
                    S.dma("pool", wb[1 - wj][:], wv[:, :, (g + 1) * 512:(g + 2) * 512], "wb%d" % (1 - wj), writes=["wb%d" % (1 - wj)])
                wn = "wb%d" % wj
                if g < 24:
                    for b, (t0, n) in enumerate(BLOCKS):
                        bk = pscount[0] % 2
                        pscount[0] += 1
                        pn = "ps%d" % bk
                        j = it % 2
                        it += 1
                        for kc in range(16):
                            S.op("pe", lambda e, kc=kc, bk=bk, wj=wj, t0=t0, n=n: e.matmul(psf(bk)[0:n, :], lhsT=hnT[:, kc, t0:t0 + n], rhs=wb[wj][:, kc, :],
                                                                                       start=(kc == 0), stop=(kc == 15)),
                                 reads=["hnT", wn], writes=[pn])
                        P = psf(bk)
                        stn, st2n, obn, ofn = "st%d" % j, "stb%d" % j, "ob%d" % j, "of%d" % j
                        if g < 4:
                            tb = 0 if g < 2 else 2
                            dst = (Qs["r"] if g < 2 else Ks["r"])[t0:t0 + n, (g % 2) * 512:(g % 2) * 512 + 512]
                            cc = rot[0:n, tb, b, :].unsqueeze(1).broadcast_to([n, 4, 128])
                            ss = rot[0:n, tb + 1, b, :].unsqueeze(1).broadcast_to([n, 4, 128])
                            P4 = P[0:n, :].rearrange("p (h c) -> p h c", h=4)
                            A4 = st[j][0:n, :].rearrange("p (h c) -> p h c", h=4)
                            B4 = st2[j][0:n, :].rearrange("p (h c) -> p h c", h=4)
                            O4 = ob[j][0:n, :].rearrange("p (h c) -> p h c", h=4)
                            S.op("dve", lambda e, A4=A4, P4=P4, cc=cc: e.tensor_tensor(A4, P4, cc, ALU.mult), reads=[pn, "rot"], writes=[stn])
                            S.op("dve", lambda e, B4=B4, P4=P4, ss=ss: e.tensor_tensor(B4, P4, ss, ALU.mult), reads=[pn, "rot"], writes=[st2n])
                            S.op("pool", lambda e, A4=A4, B4=B4, O4=O4: e.tensor_tensor(O4[:, :, 0:64], A4[:, :, 0:64], B4[:, :, 64:128], ALU.subtract),
                                 reads=[stn, st2n], writes=[obn])
                            S.op("pool", lambda e, A4=A4, B4=B4, O4=O4: e.tensor_tensor(O4[:, :, 64:128], B4[:, :, 0:64], A4[:, :, 64:128], ALU.add),
                                 reads=[stn, st2n], writes=[obn])
                            S.dma("sp", dst, ob[j][0:n, :], obn, reads=[obn], writes=["Q" if g < 2 else "K"])
                        elif g < 8 or 16 <= g < 20:
                            br = "r" if g < 8 else "h"
                            c0 = ((g - 4) if g < 8 else (g - 16)) * 512
                            S.op("act", lambda e, P=P, j=j, n=n: e.activation(out=ob[j][0:n, :], in_=P[0:n, :], func=AF.Copy), reads=[pn], writes=[obn])
                            S.dma("sp", Vs[br][t0:t0 + n, c0:c0 + 512], ob[j][0:n, :], obn, reads=[obn], writes=["V" + br])
                        elif g < 12:
                            c0 = (g - 8) * 512
                            S.op("act", lambda e, P=P, j=j, n=n: e.activation(out=ob[j][0:n, :], in_=P[0:n, :], func=AF.Silu), reads=[pn], writes=[obn])
                            S.dma("sp", Gs["r"][t0:t0 + n, c0:c0 + 512], ob[j][0:n, :], obn, reads=[obn], writes=["Gr"])
                        elif g < 14:
                            c0 = (g - 12) * 512
                            S.op("act", lambda e, P=P, j=j, n=n: e.activation(out=ob[j][0:n, :], in_=P[0:n, :], func=AF.Silu), reads=[pn], writes=[obn])
                            S.dma("sp", Qs["h"][t0:t0 + n, c0:c0 + 512], ob[j][0:n, :], obn, reads=[obn], writes=["Qh"])
                        elif g < 16:
                            c0 = (g - 14) * 512
                            S.op("act", lambda e, P=P, j=j, n=n: e.activation(out=st[j][0:n, :], in_=P[0:n, :], func=AF.Sigmoid, scale=-1.0), reads=[pn], writes=[stn])
                            S.op("dve", lambda e, j=j, n=n, c0=c0: e.tensor_tensor(st[j][0:n, :], st[j][0:n, :], oml[0:n, 0, c0:c0 + 512], ALU.mult),
                                 reads=[stn, "oml"], writes=[stn])
                            S.op("act", lambda e, j=j, n=n: e.activation(out=of[j][0:n, :], in_=st[j][0:n, :], func=AF.Ln, scale=-1.0, bias=oneT[0:n, 0:1]), reads=[stn, "oneT"], writes=[ofn])
                            S.op("pool", lambda e, j=j, n=n: e.tensor_copy(ob[j][0:n, :], st[j][0:n, :]), reads=[stn], writes=[obn])
                            S.dma("sp", Ks["h"][t0:t0 + n, c0:c0 + 512], ob[j][0:n, :], obn, reads=[obn], writes=["Kh"])
                            S.dma("sp", LFh[t0:t0 + n, c0:c0 + 512], of[j][0:n, :], ofn, reads=[ofn], writes=["LFh"])
                        else:
                            c0 = (g - 20) * 512
                            S.op("act", lambda e, P=P, j=j, n=n: e.activation(out=st[j][0:n, :], in_=P[0:n, :], func=AF.Silu), reads=[pn], writes=[stn])
                            S.op("dve", lambda e, j=j, n=n, c0=c0: e.tensor_tensor(ob[j][0:n, :], st[j][0:n, :], w8[0:n, 0, c0:c0 + 512], ALU.mult),
                                 reads=[stn, "w8"], writes=[obn])
                            S.dma("sp", Gs["h"][t0:t0 + n, c0:c0 + 512], ob[j][0:n, :], obn, reads=[obn], writes=["Gh"])
                else:
                    for cb in range(4):
                        row = (g - 24) * 4 + cb
                        for (t0, n) in tok_tiles(0, T):
                            bk = pscount[0] % 2
                            pscount[0] += 1
                            pn = "ps%d" % bk
                            j = it % 2
                            it += 1
                            for kc in range(16):
                                S.op("pe", lambda e, kc=kc, bk=bk, wj=wj, t0=t0, n=n, cb=cb: e.matmul(psf(bk)[:, 0:n], lhsT=wb[wj][:, kc, cb * 128:(cb + 1) * 128],
                                                                                                 rhs=hnT[:, kc, t0:t0 + n], start=(kc == 0), stop=(kc == 15)),
                                     reads=["hnT", wn], writes=[pn])
                            S.op("act", lambda e, bk=bk, j=j, n=n: e.activation(out=ob[j][:, 0:n], in_=psf(bk)[:, 0:n], func=AF.Sigmoid), reads=[pn], writes=["ob%d" % j])
                            S.dma("sp", GT_v[:, row, t0:t0 + n], ob[j][:, 0:n], "ob%d" % j, reads=["ob%d" % j], writes=["GT"])
            S.drain()
        S.forget(["hnT", "wb0", "wb1"] + ["%s%d" % (a, i) for a in ("st", "stb", "ob", "of") for i in range(2)])
        if stop_after == "inproj%d" % l:
            return finish()

        with ExitStack() as es:
            Sf = {br: sb(es, "Sf" + br, [128, 8, 256], F32) for br in "rh"}
            Sb = {br: sb(es, "Sb" + br, [128, 8, 256], BF16) for br in "rh"}
            for br in "rh":
                S.op("dve", lambda e, br=br: e.memset(Sf[br][:], 0.0), writes=["Sf" + br])
                S.op("pool", lambda e, br=br: e.memset(Sb[br][:], 0.0), writes=["Sb" + br])
            NB2 = 2
            Qt = {br: [sb(es, "Qt%s%d" % (br, i), [64, 1024], BF16) for i in range(NB2)] for br in "rh"}
            Kt = {br: [sb(es, "Kt%s%d" % (br, i), [64, 1024], BF16) for i in range(NB2)] for br in "rh"}
            Vt = {br: [sb(es, "Vt%s%d" % (br, i), [64, 2048], BF16) for i in range(NB2)] for br in "rh"}
            Gt = {br: [sb(es, "Gt%s%d" % (br, i), [64, 2048], BF16) for i in range(NB2)] for br in "rh"}
            Lt = [sb(es, "Lt%d" % i, [64, 1024], F32) for i in range(NB2)]
            E1 = sb(es, "E1", [128, 4, 64], F32)
            E2 = sb(es, "E2", [128, 4, 64], F32)
            EM = sb(es, "EM", [128, 4, 2], F32)
            E4 = sb(es, "E4", [64, 512], F32)
            QTt = sb(es, "QTt", [128, 4, 64], BF16)
            KTt = sb(es, "KTt", [128, 4, 64], BF16)
            QH = sb(es, "QH", [128, 4, 64], BF16)
            KE = sb(es, "KE", [64, 512], BF16)
            ATs = sb(es, "ATs", [64, 4, 64], BF16)
            SQ = sb(es, "SQ", [64, 4, 256], F32)
            SSm = sb(es, "SSm", [64, 4], F32)
            Yb = sb(es, "Yb", [64, 1024], BF16)
            YST = {br: [sb(es, "YST%s%d" % (br, i), [128, 16, 512], BF16) for i in range(2)] for br in "rh"}

            def load_sub(si):
                t0, n = SUBS[si]
                j = si % NB2
                for br in "rh":
                    S.dma("sp", Qt[br][j][0:n, :], Qs[br][t0:t0 + n, :], "Qt%s%d" % (br, j), reads=["Q" if br == "r" else "Qh"], writes=["Qt%s%d" % (br, j)])
                    S.dma("sp", Kt[br][j][0:n, :], Ks[br][t0:t0 + n, :], "Kt%s%d" % (br, j), reads=["K" if br == "r" else "Kh"], writes=["Kt%s%d" % (br, j)])
                    S.dma("sp", Vt[br][j][0:n, :], Vs[br][t0:t0 + n, :], "Vt%s%d" % (br, j), reads=["V" + br], writes=["Vt%s%d" % (br, j)])
                    S.dma("sp", Gt[br][j][0:n, :], Gs[br][t0:t0 + n, :], "Gt%s%d" % (br, j), reads=["G" + br], writes=["Gt%s%d" % (br, j)])
                S.dma("sp", Lt[j][0:n, :], LFh[t0:t0 + n, :], "Lt%d" % j, reads=["LFh"], writes=["Lt%d" % j])

            groups = tok_tiles(0, 16, 512) + tok_tiles(16, 2048, 512)
            grp_of = {}
            for gi, (g0, gn) in enumerate(groups):
                for si, (t0, n) in enumerate(SUBS):
                    if g0 <= t0 < g0 + gn:
                        grp_of[si] = gi
            EMp = [EM, sb(es, "EMb", [128, 4, 2], F32)]
            QHp = [QH, sb(es, "QHb", [128, 4, 64], BF16)]
            KEp = [KE, sb(es, "KEb", [64, 512], BF16)]
            ATp = [ATs, sb(es, "ATsb", [64, 4, 64], BF16)]
            steps = [(si, br, hh) for si in range(len(SUBS)) for br in "rh" for hh in range(2)]
            E1r = [sb(es, "E1r%d" % i, [128, 4, 64], F32) for i in range(2)]
            E2r = [sb(es, "E2r%d" % i, [128, 4, 64], F32) for i in range(2)]
            EMr = [sb(es, "EMr%d" % i, [128, 4, 2], F32) for i in range(2)]
            E4r = [sb(es, "E4r%d" % i, [64, 512], F32) for i in range(2)]

            class C:
                pass

            def ctx(k):
                c = C()
                c.si, c.br, c.hh = steps[k]
                c.t0, c.n = SUBS[c.si]
                c.j = c.si % NB2
                c.p = k % 2
                c.mm = mm16 if c.n == 16 else mm64
                c.mmn = "mm16" if c.n == 16 else "mm64"
                c.gi = grp_of[c.si]
                c.g0, c.gn = groups[c.gi]
                c.sj = c.gi % 2
                br, j = c.br, c.j
                c.LF = lfr if br == "r" else Lt[j]
                c.LFn = "lfr" if br == "r" else "Lt%d" % j
                c.Qn, c.Kn, c.Vn, c.Gn = ("Qt%s%d" % (br, j), "Kt%s%d" % (br, j), "Vt%s%d" % (br, j), "Gt%s%d" % (br, j))
                c.Q_, c.K_, c.V_, c.G_ = Qt[br][j], Kt[br][j], Vt[br][j], Gt[br][j]
                c.ystn = "YST%s%d" % (br, c.sj)
                c.c0 = c.hh * 512
                c.v0 = c.hh * 1024
                c.EM, c.QH, c.KE, c.AT = EMp[c.p], QHp[c.p], KEp[c.p], ATp[c.p]
                c.EMn, c.QHn, c.KEn, c.ATn = "EM%d" % c.p, "QH%d" % c.p, "KE%d" % c.p, "AT%d" % c.p
                c.E1, c.E2, c.E4 = E1, E2, E4
                c.E1n, c.E2n, c.E4n = "E1", "E2", "E4"
                c.comp = True
                if br == "r" and c.si >= 1:
                    hh = c.hh
                    c.E1, c.E2, c.E4, c.EM = E1r[hh], E2r[hh], E4r[hh], EMr[hh]
                    c.E1n, c.E2n, c.E4n, c.EMn = "E1r%d" % hh, "E2r%d" % hh, "E4r%d" % hh, "EMr%d" % hh
                    c.comp = (c.si == 1)
                return c

            PT = psb(2)
            PA = psf(2)[:, 256:512].rearrange("p (h c) -> p h c", h=4)
            PB = psf(3)[:, 0:264].rearrange("p (h c) -> p h c", h=4)
            PT5 = psb(5)

            def stage_a1(k):
                c = ctx(k)
                n, c0 = c.n, c.c0
                for h in range(4):
                    S.op("pe", lambda e: e.transpose(PT[:, h * 64:h * 64 + n], c.Q_[0:n, c0 + h * 128:c0 + (h + 1) * 128], ident_b[0:n, 0:n]),
                         reads=[c.Qn, "ident_b"], writes=["ps2"])
                    S.op("pe", lambda e: e.transpose(PT[:, 256 + h * 64:256 + h * 64 + n], c.K_[0:n, c0 + h * 128:c0 + (h + 1) * 128], ident_b[0:n, 0:n]),
                         reads=[c.Kn, "ident_b"], writes=["ps2"])
                if not c.comp:
                    return
                for h in range(4):
                    S.op("pe", lambda e: e.matmul(PB[:, h, 0:n + 2], lhsT=c.LF[0:n, c0 + h * 128:c0 + (h + 1) * 128], rhs=c.mm[0:n, 0:n + 2], start=True, stop=True),
                         reads=[c.LFn, c.mmn], writes=["ps3"])
                S.op("pe", lambda e: e.matmul(psf(4)[0:n, :], lhsT=m2[0:n, 0:n], rhs=c.LF[0:n, c0:c0 + 512], start=True, stop=True),
                     reads=[c.LFn, "m2"], writes=["ps4"])

            def stage_a2(k):
                c = ctx(k)
                n, c0 = c.n, c.c0
                E1, E2, E4 = c.E1, c.E2, c.E4
                if c.comp:
                    S.op("act", lambda e: e.activation(out=E1[:, :, 0:n], in_=PB[:, :, 0:n], func=AF.Exp), reads=["ps3"], writes=[c.E1n])
                    S.op("act", lambda e: e.activation(out=E2[:, :, 0:n], in_=PB[:, :, 0:n], func=AF.Exp, scale=-1.0), reads=["ps3"], writes=[c.E2n])
                    S.op("act", lambda e: e.activation(out=c.EM[:], in_=PB[:, :, n:n + 2], func=AF.Exp), reads=["ps3"], writes=[c.EMn])
                    S.op("act", lambda e: e.activation(out=E4[0:n, :], in_=psf(4)[0:n, :], func=AF.Exp), reads=["ps4"], writes=[c.E4n])
                PTq = PT[:, 0:256].rearrange("p (h c) -> p h c", h=4)[:, :, 0:n]
                PTk = PT[:, 256:512].rearrange("p (h c) -> p h c", h=4)[:, :, 0:n]
                S.op("dve", lambda e: e.tensor_tensor(QTt[:, :, 0:n], PTq, E1[:, :, 0:n], ALU.mult), reads=["ps2", c.E1n], writes=["QTt"])
                S.op("dve", lambda e: e.tensor_tensor(KTt[:, :, 0:n], PTk, E2[:, :, 0:n], ALU.mult), reads=["ps2", c.E2n], writes=["KTt"])
                S.op("pool", lambda e: e.tensor_tensor(c.QH[:, :, 0:n], QTt[:, :, 0:n], c.EM[:, :, 0:1].broadcast_to([128, 4, n]), ALU.mult),
                     reads=["QTt", c.EMn], writes=[c.QHn])
                S.op("pool", lambda e: e.tensor_tensor(c.KE[0:n, :], c.K_[0:n, c0:c0 + 512], E4[0:n, :], ALU.mult), reads=[c.Kn, c.E4n], writes=[c.KEn])
                for h in range(4):
                    S.op("pe", lambda e: e.matmul(PA[0:n, h, 0:n], lhsT=KTt[:, h, 0:n], rhs=QTt[:, h, 0:n], start=True, stop=True),
                         reads=["KTt", "QTt"], writes=["ps2"])
                S.op("dve", lambda e: e.tensor_tensor(c.AT[0:n, :, 0:n], PA[0:n, :, 0:n], mask64[0:n, 0:n].unsqueeze(1).broadcast_to([n, 4, n]), ALU.mult),
                     reads=["ps2", "mask64"], writes=[c.ATn])

            def stage_b1(k):
                c = ctx(k)
                n, v0, br, hh = c.n, c.v0, c.br, c.hh
                for h in range(4):
                    po = psf(6 + h // 2)[0:n, (h % 2) * 256:(h % 2) * 256 + 256]
                    S.op("pe", lambda e: e.matmul(po, lhsT=c.AT[0:n, h, 0:n], rhs=c.V_[0:n, v0 + h * 256:v0 + (h + 1) * 256], start=True, stop=False),
                         reads=[c.ATn, c.Vn], writes=["ps%d" % (6 + h // 2)])
                    S.op("pe", lambda e: e.matmul(po, lhsT=c.QH[:, h, 0:n], rhs=Sb[br][:, hh * 4 + h, :], start=False, stop=True),
                         reads=[c.QHn, "Sb" + br], writes=["ps%d" % (6 + h // 2)])
                for h in range(4):
                    pd = psf(h // 2)[:, (h % 2) * 256:(h % 2) * 256 + 256]
                    S.op("pe", lambda e: e.matmul(pd, lhsT=c.KE[0:n, h * 128:(h + 1) * 128], rhs=c.V_[0:n, v0 + h * 256:v0 + (h + 1) * 256], start=True, stop=True),
                         reads=[c.KEn, c.Vn], writes=["ps%d" % (h // 2)])

            Ybp = [Yb, sb(es, "Ybb", [64, 1024], BF16)]
            RSm = sb(es, "RSm", [64, 4], F32)

            def stage_b2(k):
                c = ctx(k)
                n, v0, br, hh = c.n, c.v0, c.br, c.hh
                Ybk = Ybp[c.p]
                Ybn = "Yb%d" % c.p
                for k2 in range(2):
                    S.op("act", lambda e: e.activation(out=SQ[0:n, 2 * k2:2 * k2 + 2, :], in_=psf(6 + k2)[0:n, :].rearrange("p (a c) -> p a c", a=2), func=AF.Square),
                         reads=["ps%d" % (6 + k2)], writes=["SQ"])
                S.op("dve", lambda e: e.tensor_reduce(out=SSm[0:n, :], in_=SQ[0:n, :, :], axis=AX.X, op=ALU.add), reads=["SQ"], writes=["SSm"])
                S.op("act", lambda e: e.activation(out=RSm[0:n, :], in_=SSm[0:n, :], func=AF.Ln, scale=1.0 / 256, bias=epsT[0:n, 0:1]), reads=["SSm", "epsT"], writes=["RSm"])
                S.op("act", lambda e: e.activation(out=RSm[0:n, :], in_=RSm[0:n, :], func=AF.Exp, scale=-0.5), reads=["RSm"], writes=["RSm"])
                for h in range(4):
                    S.op("dve", lambda e: e.scalar_tensor_tensor(out=Ybk[0:n, h * 256:(h + 1) * 256], in0=psf(6 + h // 2)[0:n, (h % 2) * 256:(h % 2) * 256 + 256],
                                                                 scalar=RSm[0:n, h:h + 1], in1=c.G_[0:n, v0 + h * 256:v0 + (h + 1) * 256], op0=ALU.mult, op1=ALU.mult),
                         reads=["ps%d" % (6 + h // 2), "RSm", c.Gn], writes=[Ybn])
                for h in range(4):
                    S.op("dve", lambda e: e.scalar_tensor_tensor(out=Sf[br][:, hh * 4 + h, :], in0=Sf[br][:, hh * 4 + h, :], scalar=c.EM[:, h, 1:2],
                                                                 in1=psf(h // 2)[:, (h % 2) * 256:(h % 2) * 256 + 256], op0=ALU.mult, op1=ALU.add),
                         reads=["Sf" + br, c.EMn, "ps%d" % (h // 2)], writes=["Sf" + br])
                S.op("act", lambda e: e.activation(out=Sb[br][:, hh * 4:hh * 4 + 4, :], in_=Sf[br][:, hh * 4:hh * 4 + 4, :], func=AF.Copy),
                     reads=["Sf" + br], writes=["Sb" + br])

            def stage_c(k):
                c = ctx(k)
                n, br, hh, t0, g0, gn, sj = c.n, c.br, c.hh, c.t0, c.g0, c.gn, c.sj
                Ybk = Ybp[c.p]
                Ybn = "Yb%d" % c.p
                for c8 in range(8):
                    S.op("pe", lambda e: e.transpose(PT5[:, c8 * 64:c8 * 64 + n], Ybk[0:n, c8 * 128:(c8 + 1) * 128], ident_b[0:n, 0:n]),
                         reads=[Ybn, "ident_b"], writes=["ps5"])
                S.op("act", lambda e: e.activation(out=YST[br][sj][:, hh * 8:hh * 8 + 8, t0 - g0:t0 - g0 + n],
                                                   in_=PT5[:, 0:512].rearrange("p (a c) -> p a c", a=8)[:, :, 0:n], func=AF.Copy),
                     reads=["ps5"], writes=[c.ystn])
                if hh == 1 and t0 + n == g0 + gn:
                    S.dma("sp", yT_v[br][:, :, g0:g0 + gn], YST[br][sj][:, :, 0:gn], c.ystn, reads=[c.ystn], writes=["yT" + br])

            load_sub(0)
            load_sub(1)
            stage_a1(0)
            stage_a2(0)
            NS = len(steps)
            for k in range(NS):
                if k + 1 < NS:
                    stage_a1(k + 1)
                stage_b1(k)
                if k + 1 < NS:
                    stage_a2(k + 1)
                stage_b2(k)
                if k >= 1:
                    stage_c(k - 1)
                    si_, br_, hh_ = steps[k - 1]
                    if (br_, hh_) == ("h", 1) and si_ + 2 < len(SUBS):
                        load_sub(si_ + 2)
            stage_c(NS - 1)
            S.drain()
        S.forget([k for k in list(S.last_w.keys()) if isinstance(k, str) and (k[:2] in ("Qt", "Kt", "Vt", "Gt", "Lt", "YS", "Sf", "Sb") or
                                                                             k in ("E1", "E2", "EM0", "EM1", "E4", "QTt", "KTt", "QH0", "QH1", "KE0", "KE1", "AT0", "AT1", "TMPS", "SQ", "SSm", "Yf", "Yb0", "Yb1", "RSm"))])
        if stop_after == "mixer%d" % l:
            return finish()

        for hi_, (h0, hn_) in enumerate(HALVES):
            tiles = tok_tiles(h0, hn_)
            with ExitStack() as es:
                with ExitStack() as es2:
                    yTs = sb(es2, "yTs", [128, 16, 1040], BF16)
                    with ExitStack() as es3:
                        yin = {br: sb(es3, "yin" + br, [128, 16, 1040], BF16) for br in "rh"}
                        for br in "rh":
                            S.dma("sp", yin[br][:, :, 0:hn_], yT_v[br][:, :, h0:h0 + hn_], "yin" + br, reads=["yT" + br], writes=["yin" + br])
                        wr = {br: [sb(es3, "wr%s%d" % (br, i), [128, 16, 256], BF16) for i in range(2)] for br in "rh"}
                        gt = {br: [sb(es3, "gt%s%d" % (br, i), [128, 512], BF16) for i in range(2)] for br in "rh"}
                        t1 = [sb(es3, "t1_%d" % i, [128, 512], F32) for i in range(2)]
                        t2 = [sb(es3, "t2_%d" % i, [128, 512], F32) for i in range(2)]
                        wsrc = {"r": w_br_ret[l].rearrange("(kc p) n -> p kc n", p=128), "h": w_br_hg[l].rearrange("(kc p) n -> p kc n", p=128)}
                        for br in "rh":
                            S.dma("pool", wr[br][0][:], wsrc[br][:, :, 0:256], "wr%s0" % br, writes=["wr%s0" % br])
                        it = 0
                        for g in range(8):
                            wj = g % 2
                            if g + 1 < 8:
                                for br in "rh":
                                    S.dma("pool", wr[br][1 - wj][:], wsrc[br][:, :, (g + 1) * 256:(g + 2) * 256], "wr%s%d" % (br, 1 - wj), writes=["wr%s%d" % (br, 1 - wj)])
                            for cb in range(2):
                                row = g * 2 + cb
                                for (t0, n) in tiles:
                                    j = it % 2
                                    it += 1
                                    bks = {"r": 2 * j, "h": 2 * j + 1}
                                    for bi, br in enumerate("rh"):
                                        S.dma("sp", gt[br][j][:, 0:n], GT_v[:, bi * 16 + row, t0:t0 + n], "gt%s%d" % (br, j), reads=["GT"], writes=["gt%s%d" % (br, j)])
                                        bk = bks[br]
                                        for kc in range(16):
                                            S.op("pe", lambda e, kc=kc, bk=bk, br=br, wj=wj, cb=cb, t0=t0, n=n: e.matmul(
                                                psf(bk)[:, 0:n], lhsT=wr[br][wj][:, kc, cb * 128:(cb + 1) * 128], rhs=yin[br][:, kc, t0 - h0:t0 - h0 + n],
                                                start=(kc == 0), stop=(kc == 15)), reads=["yin" + br, "wr%s%d" % (br, wj)], writes=["ps%d" % bk])
                                    S.op("dve", lambda e, j=j, n=n, bk=bks["r"]: e.tensor_tensor(t1[j][:, 0:n], psf(bk)[:, 0:n], gt["r"][j][:, 0:n], ALU.mult),
                                         reads=["ps%d" % bks["r"], "gtr%d" % j], writes=["t1_%d" % j])
                                    S.op("dve", lambda e, j=j, n=n, bk=bks["h"]: e.tensor_tensor(t2[j][:, 0:n], psf(bk)[:, 0:n], gt["h"][j][:, 0:n], ALU.mult),
                                         reads=["ps%d" % bks["h"], "gth%d" % j], writes=["t2_%d" % j])
                                    S.op("pool", lambda e, j=j, n=n, row=row, t0=t0: e.tensor_tensor(yTs[:, row, t0 - h0:t0 - h0 + n], t1[j][:, 0:n], t2[j][:, 0:n], ALU.add),
                                         reads=["t1_%d" % j, "t2_%d" % j], writes=["yTs"])
                        S.drain()
                    S.forget(["yinr", "yinh"] + ["%s%s%d" % (a, br, i) for a in ("wr", "gt") for br in "rh" for i in range(2)] + ["t1_0", "t1_1", "t2_0", "t2_1"])
                    with ExitStack() as es3:
                        wo = [sb(es3, "wo%d" % i, [128, 16, 512], BF16) for i in range(2)]
                        mo = [sb(es3, "mo%d" % i, [128, 512], F32) for i in range(2)]
                        wsrc = w_out[l].rearrange("(kc p) n -> p kc n", p=128)
                        S.dma("pool", wo[0][:], wsrc[:, :, 0:512], "wo0", writes=["wo0"])
                        it = 0
                        for g in range(4):
                            wj = g % 2
                            if g + 1 < 4:
                                S.dma("pool", wo[1 - wj][:], wsrc[:, :, (g + 1) * 512:(g + 2) * 512], "wo%d" % (1 - wj), writes=["wo%d" % (1 - wj)])
                            for cb in range(4):
                                row = g * 4 + cb
                                for (t0, n) in tiles:
                                    j = it % 2
                                    it += 1
                                    bk = j
                                    for kc in range(16):
                                        S.op("pe", lambda e, kc=kc, bk=bk, wj=wj, cb=cb, t0=t0, n=n: e.matmul(
                                            psf(bk)[:, 0:n], lhsT=wo[wj][:, kc, cb * 128:(cb + 1) * 128], rhs=yTs[:, kc, t0 - h0:t0 - h0 + n],
                                            start=(kc == 0), stop=(kc == 15)), reads=["yTs", "wo%d" % wj], writes=["ps%d" % bk])
                                    S.op("act", lambda e, j=j, n=n, bk=bk: e.activation(out=mo[j][:, 0:n], in_=psf(bk)[:, 0:n], func=AF.Copy), reads=["ps%d" % bk], writes=["mo%d" % j])
                                    S.dma("sp", mT_v[:, row, t0:t0 + n], mo[j][:, 0:n], "mo%d" % j, reads=["mo%d" % j], writes=["mT"])
                        S.drain()
                    S.forget(["wo0", "wo1", "mo0", "mo1", "yTs"])
                if stop_after == "p4_%d_%d" % (l, hi_):
                    return finish()
                hn2T = sb(es, "hn2T", [128, 16, 1040], BF16)
                norm_pass(l, [(h0, hn_)], 1, 2, hn2T, "hn2T", h0, "p5")
                if stop_after == "p5_%d_%d" % (l, hi_):
                    return finish()
                with ExitStack() as es2:
                    hidT = sb(es2, "hidT", [128, KCF, 1040], BF16)
                    with ExitStack() as es3:
                        wg = {m_: [sb(es3, "wg%s%d" % (m_, i), [128, 16, 256], BF16) for i in range(2)] for m_ in "gu"}
                        sg = [sb(es3, "sg%d" % i, [128, 512], F32) for i in range(2)]
                        wsrc = {"g": w_gate[l].rearrange("(kc p) n -> p kc n", p=128), "u": w_up[l].rearrange("(kc p) n -> p kc n", p=128)}
                        NGF = DFF // 256
                        for m_ in "gu":
                            S.dma("pool", wg[m_][0][:], wsrc[m_][:, :, 0:256], "wg%s0" % m_, writes=["wg%s0" % m_])
                        it = 0
                        for g in range(NGF):
                            wj = g % 2
                            if g + 1 < NGF:
                                for m_ in "gu":
                                    S.dma("pool", wg[m_][1 - wj][:], wsrc[m_][:, :, (g + 1) * 256:(g + 2) * 256], "wg%s%d" % (m_, 1 - wj), writes=["wg%s%d" % (m_, 1 - wj)])
                            for cb in range(2):
                                fc = g * 2 + cb
                                for (t0, n) in tiles:
                                    j = it % 2
                                    it += 1
                                    bks = {"g": 2 * j, "u": 2 * j + 1}
                                    for m_ in "gu":
                                        bk = bks[m_]
                                        for kc in range(16):
                                            S.op("pe", lambda e, kc=kc, bk=bk, m_=m_, wj=wj, cb=cb, t0=t0, n=n: e.matmul(
                                                psf(bk)[:, 0:n], lhsT=wg[m_][wj][:, kc, cb * 128:(cb + 1) * 128], rhs=hn2T[:, kc, t0 - h0:t0 - h0 + n],
                                                start=(kc == 0), stop=(kc == 15)), reads=["hn2T", "wg%s%d" % (m_, wj)], writes=["ps%d" % bk])
                                    S.op("act", lambda e, j=j, n=n, bk=bks["g"]: e.activation(out=sg[j][:, 0:n], in_=psf(bk)[:, 0:n], func=AF.Silu),
                                         reads=["ps%d" % bks["g"]], writes=["sg%d" % j])
                                    S.op("dve", lambda e, j=j, n=n, bk=bks["u"], fc=fc, t0=t0: e.tensor_tensor(hidT[:, fc, t0 - h0:t0 - h0 + n], psf(bk)[:, 0:n], sg[j][:, 0:n], ALU.mult),
                                         reads=["ps%d" % bks["u"], "sg%d" % j], writes=["hidT"])
                        S.drain()
                    S.forget(["wgg0", "wgg1", "wgu0", "wgu1", "sg0", "sg1"])
                    with ExitStack() as es3:
                        wd = [sb(es3, "wd%d" % i, [128, KCF, 128], BF16) for i in range(2)]
                        mo = [sb(es3, "fo%d" % i, [128, 512], F32) for i in range(2)]
                        wsrc = w_down[l].rearrange("(kc p) n -> p kc n", p=128)
                        S.dma("pool", wd[0][:], wsrc[:, :, 0:128], "wd0", writes=["wd0"])
                        it = 0
                        for cbk in range(16):
                            wj = cbk % 2
                            if cbk + 1 < 16:
                                S.dma("pool", wd[1 - wj][:], wsrc[:, :, (cbk + 1) * 128:(cbk + 2) * 128], "wd%d" % (1 - wj), writes=["wd%d" % (1 - wj)])
                            for (t0, n) in tiles:
                                j = it % 2
                                it += 1
                                bk = j
                                for kc in range(KCF):
                                    S.op("pe", lambda e, kc=kc, bk=bk, wj=wj, t0=t0, n=n: e.matmul(
                                        psf(bk)[:, 0:n], lhsT=wd[wj][:, kc, :], rhs=hidT[:, kc, t0 - h0:t0 - h0 + n],
                                        start=(kc == 0), stop=(kc == KCF - 1)), reads=["hidT", "wd%d" % wj], writes=["ps%d" % bk])
                                S.op("act", lambda e, j=j, n=n, bk=bk: e.activation(out=mo[j][:, 0:n], in_=psf(bk)[:, 0:n], func=AF.Copy), reads=["ps%d" % bk], writes=["fo%d" % j])
                                S.dma("sp", mT_v[:, cbk, t0:t0 + n], mo[j][:, 0:n], "fo%d" % j, reads=["fo%d" % j], writes=["mT"])
                        S.drain()
                    S.forget(["wd0", "wd1", "fo0", "fo1", "hidT"])
                S.forget(["hn2T"])
            norm_pass(l, [(h0, hn_)], 3, None, None, None, 0, "p7")
        if stop_after == "layer%d" % l:
            return finish()

    with ExitStack() as es:
        hi2 = [sb(es, "hi2_%d" % i, [128, 16, 128], F32) for i in range(2)]
        oo = [sb(es, "oo%d" % i, [128, D], F32) for i in range(2)]
        for b, (t0, n) in enumerate(BLOCKS):
            if b == 0:
                continue
            j = b % 2
            S.dma("sp", hi2[j][:], hT_v[:, :, t0:t0 + n], "hi2_%d" % j, reads=["hT"], writes=["hi2_%d" % j])
            for q4 in range(4):
                bk = pscount[0] % 2
                pscount[0] += 1
                for i4 in range(4):
                    kc = q4 * 4 + i4
                    S.op("pe", lambda e, kc=kc, i4=i4, bk=bk, j=j: e.transpose(psf(bk)[:, i4 * 128:(i4 + 1) * 128], hi2[j][:, kc, :], ident_f[:]),
                         reads=["hi2_%d" % j, "ident_f"], writes=["ps%d" % bk])
                if q4 % 2:
                    S.op("act", lambda e, bk=bk, q4=q4, j=j: e.activation(out=oo[j][:, q4 * 512:(q4 + 1) * 512], in_=psf(bk)[:], func=AF.Copy), reads=["ps%d" % bk], writes=["oo%d" % j])
                else:
                    S.op("dve", lambda e, bk=bk, q4=q4, j=j: e.tensor_copy(oo[j][:, q4 * 512:(q4 + 1) * 512], psf(bk)[:]), reads=["ps%d" % bk], writes=["oo%d" % j])
            S.dma("sp", out[t0 - 16:t0 - 16 + n, :], oo[j][:], "oo%d" % j, reads=["oo%d" % j], writes=["out"])
    return finish()


_NC_CACHE = {}


def kernel(**inputs):
    names = ["meta_tokens", "norm_mix_pre", "norm_mix_post", "norm_ffn_pre", "norm_ffn_post", "w_in", "hg_lb_logits",
             "hg_norm_w", "w_br_ret", "w_br_hg", "w_out", "w_ffn_gate", "w_ffn_up", "w_ffn_down"]
    x = np.ascontiguousarray(np.asarray(inputs["x"], dtype=np.float32))
    shared = {n: np.ascontiguousarray(np.asarray(inputs[n], dtype=np.float32)) for n in names}
    if "nc" not in _NC_CACHE:
        _NC_CACHE["nc"] = build_nc()
    nc = _NC_CACHE["nc"]
    in_maps = []
    for c in range(8):
        m = dict(shared)
        m["x"] = x[c]
        in_maps.append(m)
    res = run_bass_kernel_spmd(nc, in_maps, core_ids=list(range(8)))
    return np.stack([np.asarray(r["out"], dtype=np.float32) for r in res.results], axis=0)
```

```python
import math
import numpy as np
import concourse.bass as bass
import concourse.mybir as mybir
from concourse.bass_utils import run_bass_kernel_spmd

F32 = mybir.dt.float32
BF16 = mybir.dt.bfloat16
AF = mybir.ActivationFunctionType
ALU = mybir.AluOpType
AX = mybir.AxisListType

D = 2048
T = 2064
NMETA = 16
DFF = 5632
KCF = DFF // 128
DEPTH = 2
EPS = 1e-6
INC = 16384
LOGG = [math.log1p(-2.0 ** (-5 - h)) for h in range(8)]

ENGS = ["pe", "act", "dve", "pool", "sp"]
EPOCH = 16000
SAME_ENGINE_SYNC = {"pe": False, "act": True, "dve": True, "pool": True, "sp": False}


class _Rec:
    def __init__(self):
        self.call = None

    def __getattr__(self, name):
        def f(*a, **k):
            self.call = (name, a, k)
            return None
        return f


class Sched:
    def __init__(self, nc):
        self.nc = nc
        self.q = {e: [] for e in ENGS}
        self.cnt = {e: 0 for e in ENGS}
        self.seen = {e: {} for e in ENGS}
        self.last_w = {}
        self.readers = {}
        self.sem_handles = {}
        self.dma_cnt = {}

    def sem(self, key):
        h = self.sem_handles.get(key)
        if h is None:
            name = "s" + str(len(self.sem_handles))
            h = self.nc.alloc_semaphore(name)
            self.sem_handles[key] = h
        return h

    def _deps(self, eng, reads, writes, force_same=False):
        deps = []
        for b in reads:
            t = self.last_w.get(b)
            if t is not None:
                deps.append(t)
        for b in writes:
            t = self.last_w.get(b)
            if t is not None:
                deps.append(t)
            deps.extend(self.readers.get(b, ()))
        waits = {}
        seen = self.seen[eng]
        for (k, v) in deps:
            if k[0] == eng and not (SAME_ENGINE_SYNC[eng] or force_same):
                continue
            if seen.get(k, 0) >= v:
                continue
            if waits.get(k, 0) < v:
                waits[k] = v
        for k, v in waits.items():
            seen[k] = v
        return list(waits.items())

    def _commit(self, tok, reads, writes):
        for b in writes:
            self.last_w[b] = tok
            self.readers[b] = []
        for b in reads:
            self.readers.setdefault(b, []).append(tok)

    def op(self, eng, fn, reads=(), writes=()):
        rec = _Rec()
        fn(rec)
        fn = rec.call
        waits = self._deps(eng, reads, writes)
        c = self.cnt[eng]
        key = (eng, c // EPOCH)
        val = c % EPOCH + 1
        self.cnt[eng] = c + 1
        self.sem(key)
        self.q[eng].append((waits, fn, key, 1))
        self._commit((key, val), reads, writes)

    def dma(self, eng, out, in_, semkey, reads=(), writes=(), **kw):
        waits = self._deps(eng, reads, writes, force_same=True)
        key = ("d", semkey)
        n = self.dma_cnt.get(key, 0) + 1
        self.dma_cnt[key] = n
        self.sem(key)
        self.q[eng].append((waits, lambda e: e.dma_start(out=out, in_=in_, **kw), key, 16))
        self._commit((key, 16 * n), reads, writes)

    def drain(self):
        best = {}
        for t in list(self.last_w.values()) + [x for ts in self.readers.values() for x in ts]:
            k, v = t
            if best.get(k, 0) < v:
                best[k] = v
        for e in ENGS:
            waits = []
            for k, v in best.items():
                if self.seen[e].get(k, 0) >= v:
                    continue
                self.seen[e][k] = v
                waits.append((k, v))
            if waits:
                self.q[e].append((waits, None, None, 0))

    def forget(self, names):
        for n in names:
            self.last_w.pop(n, None)
            self.readers.pop(n, None)

    def replay(self, block):
        handles = self.sem_handles

        def run(engname):
            def body(e):
                for waits, fn, key, inc in self.q[engname]:
                    for k, v in waits:
                        e.wait_ge(handles[k], v)
                    if fn is not None:
                        if isinstance(fn, tuple):
                            ins = getattr(e, fn[0])(*fn[1], **fn[2])
                        else:
                            ins = fn(e)
                        ins.then_inc(handles[key], inc)
            return body

        block.tensor(run("pe"))
        block.scalar(run("act"))
        block.vector(run("dve"))
        block.gpsimd(run("pool"))
        block.sync(run("sp"))


def tok_tiles(t0, n, step=344):
    out = []
    a = t0
    while a < t0 + n:
        m = min(step, t0 + n - a)
        out.append((a, m))
        a += m
    return out


BLOCKS = [(0, 16)] + [(16 + 128 * i, 128) for i in range(16)]
SUBS = [(0, 16)] + [(16 + 64 * i, 64) for i in range(32)]
HALVES = [(0, 1032), (1032, 1032)]


def build_nc(debug=False, stop_after=None):
    nc = bass.Bass("TRN2", target_bir_lowering=False)
    S = Sched(nc)
    okind = "ExternalOutput" if debug else "Internal"

    def dram_in(name, shape):
        return nc.dram_tensor(name, shape, F32, kind="ExternalInput").ap()

    def scratch(name, shape, dt):
        if debug:
            return nc.dram_tensor(name, shape, dt, kind="ExternalOutput").ap()
        return nc.dram_tensor(name, shape, dt).ap()

    x = dram_in("x", [2048, D])
    meta = dram_in("meta_tokens", [NMETA, D])
    g_mix_pre = dram_in("norm_mix_pre", [DEPTH, D])
    g_mix_post = dram_in("norm_mix_post", [DEPTH, D])
    g_ffn_pre = dram_in("norm_ffn_pre", [DEPTH, D])
    g_ffn_post = dram_in("norm_ffn_post", [DEPTH, D])
    w_in = dram_in("w_in", [DEPTH, D, INC])
    lb_logits = dram_in("hg_lb_logits", [DEPTH, 1024])
    hg_norm_w = dram_in("hg_norm_w", [DEPTH, 256])
    w_br_ret = dram_in("w_br_ret", [DEPTH, D, D])
    w_br_hg = dram_in("w_br_hg", [DEPTH, D, D])
    w_out = dram_in("w_out", [DEPTH, D, D])
    w_gate = dram_in("w_ffn_gate", [DEPTH, D, DFF])
    w_up = dram_in("w_ffn_up", [DEPTH, D, DFF])
    w_down = dram_in("w_ffn_down", [DEPTH, DFF, D])
    out = nc.dram_tensor("out", [2048, D], F32, kind="ExternalOutput").ap()

    hT = scratch("hT", [D, T], F32)
    mT = scratch("mT", [D, T], F32)
    GT = scratch("GT", [2 * D, T], BF16)
    yT = {"r": scratch("yrT", [D, T], BF16), "h": scratch("yhT", [D, T], BF16)}
    Qs = {"r": scratch("Qr", [T, 1024], BF16), "h": scratch("Qh", [T, 1024], BF16)}
    Ks = {"r": scratch("Kr", [T, 1024], BF16), "h": scratch("Kh", [T, 1024], BF16)}
    Vs = {"r": scratch("Vr", [T, 2048], BF16), "h": scratch("Vh", [T, 2048], BF16)}
    Gs = {"r": scratch("Gr", [T, 2048], BF16), "h": scratch("Gh", [T, 2048], BF16)}
    LFh = scratch("LFh", [T, 1024], F32)

    def fm(ap):
        return ap.rearrange("(kc p) t -> p kc t", p=128)

    hT_v, mT_v, GT_v = fm(hT), fm(mT), fm(GT)
    yT_v = {k: fm(v) for k, v in yT.items()}

    from contextlib import ExitStack
    es_all = ExitStack()

    uniq = [0]

    def sb(es, name, shape, dt):
        uniq[0] += 1
        return es.enter_context(nc.sbuf_tensor("%s_u%d" % (name, uniq[0]), shape, dt))

    PS = [es_all.enter_context(nc.psum_tensor("psb%d" % i, [128, 512], F32)) for i in range(8)]

    def psf(i):
        return PS[i]

    def psb(i):
        return PS[i][:].bitcast(BF16)

    cs = es_all
    ident_b = sb(cs, "ident_b", [128, 128], BF16)
    ident_f = sb(cs, "ident_f", [128, 128], F32)
    ones_b = sb(cs, "ones_b", [128, 128], BF16)
    mask64 = sb(cs, "mask64", [64, 64], F32)
    mm64 = sb(cs, "mm64", [64, 66], F32)
    mm16 = sb(cs, "mm16", [64, 18], F32)
    m2 = sb(cs, "m2", [64, 64], F32)
    lfr = sb(cs, "lfr", [64, 1024], F32)
    epsT = sb(cs, "epsT", [128, 1], F32)
    piT = sb(cs, "piT", [128, 1], F32)
    gvec = sb(cs, "gvec", [128, 4, DEPTH, 16], F32)
    oneT = sb(cs, "oneT", [128, 1], F32)
    rot_d = scratch("rot_d", [128, 4 * 17 * 128], F32)
    oml_d = scratch("oml_d", [128, DEPTH * 1024], F32)
    w8_d = scratch("w8_d", [128, DEPTH * 2048], F32)

    with ExitStack() as es:
        oml = sb(es, "oml", [128, DEPTH, 1024], F32)
        w8 = sb(es, "w8", [128, DEPTH, 2048], F32)
        rot = sb(es, "rot", [128, 4, 17, 128], F32)
        dmat = sb(es, "dmat", [128, 128], F32)
        pcol = sb(es, "pcol", [128, 128], F32)
        tmpa = sb(es, "tmpa", [128, 128], F32)
        tmpb = sb(es, "tmpb", [128, 128], F32)
        pos = sb(es, "pos", [128, 17], F32)
        jj = sb(es, "jj", [128, 64], F32)
        inv = sb(es, "inv", [128, 64], F32)
        ang = sb(es, "ang", [128, 17, 64], F32)
        ang2 = sb(es, "ang2", [128, 17, 64], F32)
        lbt = sb(es, "lbt", [128, DEPTH, 1024], F32)

        S.op("pool", lambda e: e.iota(dmat[:], pattern=[[1, 128]], base=0, channel_multiplier=-1,
                                      allow_small_or_imprecise_dtypes=True), writes=["dmat"])
        S.op("pool", lambda e: e.iota(pcol[:], pattern=[[0, 128]], base=0, channel_multiplier=1,
                                      allow_small_or_imprecise_dtypes=True), writes=["pcol"])
        S.op("pool", lambda e: e.iota(pos[:], pattern=[[128, 17]], base=-112, channel_multiplier=1,
                                      allow_small_or_imprecise_dtypes=True), writes=["pos"])
        S.op("pool", lambda e: e.iota(pos[:, 0:1], pattern=[[0, 1]], base=0, channel_multiplier=1,
                                      allow_small_or_imprecise_dtypes=True), reads=["pos"], writes=["pos"])
        S.op("pool", lambda e: e.iota(jj[:], pattern=[[1, 64]], base=0, channel_multiplier=0,
                                      allow_small_or_imprecise_dtypes=True), writes=["jj"])
        S.op("dve", lambda e: e.tensor_single_scalar(ident_f[:], dmat[:], 0.0, ALU.is_equal), reads=["dmat"], writes=["ident_f"])
        S.op("dve", lambda e: e.tensor_copy(ident_b[:], ident_f[:]), reads=["ident_f"], writes=["ident_b"])
        S.op("dve", lambda e: e.memset(ones_b[:], 1.0), writes=["ones_b"])
        S.op("dve", lambda e: e.memset(epsT[:], EPS), writes=["epsT"])
        S.op("dve", lambda e: e.memset(piT[:], math.pi), writes=["piT"])
        S.op("dve", lambda e: e.memset(oneT[:], 1.0), writes=["oneT"])
        S.op("dve", lambda e: e.tensor_single_scalar(mask64[:], dmat[0:64, 0:64], 0.0, ALU.is_ge), reads=["dmat"], writes=["mask64"])
        S.op("dve", lambda e: e.tensor_single_scalar(m2[:], dmat[0:64, 0:64], 0.0, ALU.is_lt), reads=["dmat"], writes=["m2"])
        for (mm, n, mid) in ((mm64, 64, 31), (mm16, 16, 7)):
            nm = "mm%d" % n
            S.op("dve", lambda e, mid=mid: e.tensor_single_scalar(tmpa[:], pcol[:], float(mid), ALU.is_le), reads=["pcol"], writes=["tmpa"])
            S.op("dve", lambda e: e.tensor_single_scalar(tmpb[:], dmat[:], 0.0, ALU.is_ge), reads=["dmat"], writes=["tmpb"])
            S.op("dve", lambda e, mm=mm, n=n: e.tensor_tensor(mm[0:64, 0:n], tmpb[0:64, 0:n], tmpa[0:64, 0:n], ALU.subtract),
                 reads=["tmpa", "tmpb"], writes=[nm])
            S.op("dve", lambda e, mm=mm, n=n: e.tensor_copy(mm[0:64, n:n + 1], tmpa[0:64, 0:1]), reads=["tmpa"], writes=[nm])
            S.op("dve", lambda e, mm=mm, n=n: e.memset(mm[0:64, n + 1:n + 2], 1.0), writes=[nm])
        for h in range(8):
            S.op("pool", lambda e, h=h: e.memset(lfr[:, h * 128:(h + 1) * 128], LOGG[h]), writes=["lfr"])
        for wi, g in enumerate((g_mix_pre, g_mix_post, g_ffn_pre, g_ffn_post)):
            for l in range(DEPTH):
                S.dma("sp", gvec[:, wi, l, :], g[l].rearrange("(kc p) -> p kc", p=128), "gvec", writes=["gvec"],
                      allow_slow_non_contiguous=True)
        for l in range(DEPTH):
            S.dma("sp", lbt[:, l, :], lb_logits[l:l + 1, :].broadcast_to([128, 1024]), "lbt", writes=["lbt"])
            for j in range(8):
                S.dma("sp", w8[:, l, j * 256:(j + 1) * 256], hg_norm_w[l:l + 1, :].broadcast_to([128, 256]), "w8", writes=["w8"])
        S.op("dve", lambda e: e.memset(oml[:, 0, :], 1.0), writes=["oml"])
        S.op("dve", lambda e: e.tensor_tensor(lbt[:, 0, :], lbt[:, 0, :], lbt[:, 1, :], ALU.subtract), reads=["lbt"], writes=["lbt"])
        S.op("act", lambda e: e.activation(out=oml[:, 1, :], in_=lbt[:, 0, :], func=AF.Sigmoid), reads=["lbt"], writes=["oml"])
        for j_ in range(64):
            S.op("pool", lambda e, j_=j_: e.memset(inv[:, j_:j_ + 1], float(np.float32(10000.0 ** (-j_ / 64.0)))), writes=["inv"])
        S.op("dve", lambda e: e.tensor_tensor(ang[:], pos[:].unsqueeze(2).broadcast_to([128, 17, 64]),
                                              inv[:].unsqueeze(1).broadcast_to([128, 17, 64]), ALU.mult),
             reads=["pos", "inv"], writes=["ang"])
        S.op("dve", lambda e: e.tensor_single_scalar(ang2[:], ang[:], math.pi / 2, ALU.add), reads=["ang"], writes=["ang2"])
        angi = sb(es, "angi", [128, 17, 64], mybir.dt.int32)
        angk = sb(es, "angk", [128, 17, 64], F32)
        for (a_, an_) in ((ang, "ang"), (ang2, "ang2")):
            S.op("dve", lambda e, a_=a_: e.tensor_single_scalar(angk[:], a_[:], 1.0 / (2 * math.pi), ALU.mult), reads=[an_], writes=["angk"])
            S.op("dve", lambda e: e.tensor_copy(angi[:], angk[:]), reads=["angk"], writes=["angi"])
            S.op("dve", lambda e: e.tensor_copy(angk[:], angi[:]), reads=["angi"], writes=["angk"])
            S.op("dve", lambda e, a_=a_: e.scalar_tensor_tensor(out=a_[:], in0=angk[:], scalar=-2 * math.pi, in1=a_[:], op0=ALU.mult, op1=ALU.add),
                 reads=["angk", an_], writes=[an_])
            S.op("dve", lambda e, a_=a_: e.tensor_single_scalar(angk[:], a_[:], math.pi, ALU.is_gt), reads=[an_], writes=["angk"])
            S.op("dve", lambda e, a_=a_: e.scalar_tensor_tensor(out=a_[:], in0=angk[:], scalar=-2 * math.pi, in1=a_[:], op0=ALU.mult, op1=ALU.add),
                 reads=["angk", an_], writes=[an_])
            S.op("dve", lambda e, a_=a_: e.tensor_single_scalar(angk[:], a_[:], -math.pi, ALU.is_lt), reads=[an_], writes=["angk"])
            S.op("dve", lambda e, a_=a_: e.scalar_tensor_tensor(out=a_[:], in0=angk[:], scalar=2 * math.pi, in1=a_[:], op0=ALU.mult, op1=ALU.add),
                 reads=["angk", an_], writes=[an_])
            S.op("dve", lambda e, a_=a_: e.tensor_scalar(a_[:], a_[:], -3.14159, 3.14159, ALU.max, ALU.min), reads=[an_], writes=[an_])
        for half in range(2):
            S.op("act", lambda e, half=half: e.activation(out=rot[:, 0, :, half * 64:(half + 1) * 64], in_=ang2[:], func=AF.Sin),
                 reads=["ang2"], writes=["rot"])
            S.op("act", lambda e, half=half: e.activation(out=rot[:, 1, :, half * 64:(half + 1) * 64], in_=ang[:], func=AF.Sin),
                 reads=["ang"], writes=["rot"])
        S.op("dve", lambda e: e.tensor_single_scalar(rot[:, 2:4].rearrange("p a b c -> p (a b c)"), rot[:, 0:2].rearrange("p a b c -> p (a b c)"),
                                                     128.0 ** -0.5, ALU.mult), reads=["rot"], writes=["rot"])
        S.dma("sp", rot_d, rot[:].rearrange("p a b c -> p (a b c)"), "rot", reads=["rot"], writes=["rot_d"])
        S.dma("sp", oml_d, oml[:].rearrange("p a b -> p (a b)"), "oml", reads=["oml"], writes=["oml_d"])
        S.dma("sp", w8_d, w8[:].rearrange("p a b -> p (a b)"), "w8", reads=["w8"], writes=["w8_d"])
        S.drain()
    S.forget(["dmat", "pcol", "tmpa", "tmpb", "pos", "jj", "inv", "ang", "ang2", "lbt"])

    pscount = [0]

    def wload(es_name, dst, src_view, key):
        S.dma("pool", dst, src_view, key, writes=[key])

    def norm_pass(l, toks, m_gain, pre_gain, XT, xt_name, xt_t0, tag):
        with ExitStack() as es:
            NT = 344
            ht = [sb(es, "np_ht%d" % i, [128, 16, NT], F32) for i in range(2)]
            mt = [sb(es, "np_mt%d" % i, [128, 16, NT], F32) for i in range(2)] if m_gain is not None else [None, None]
            sq = sb(es, "np_sq", [128, 16, NT], BF16)
            tmp = sb(es, "np_tmp", [128, 16, NT], F32)
            rs = sb(es, "np_rs", [128, NT], F32)
            tiles = []
            for (t0, n) in toks:
                tiles += tok_tiles(t0, n, NT)
            for i, (t0, n) in enumerate(tiles):
                j = i % 2
                hb, mb = "np_ht%d" % j, "np_mt%d" % j
                S.dma("sp", ht[j][:, :, 0:n], hT_v[:, :, t0:t0 + n], hb, reads=["hT"], writes=[hb])
                if m_gain is not None:
                    S.dma("sp", mt[j][:, :, 0:n], mT_v[:, :, t0:t0 + n], mb, reads=["mT"], writes=[mb])

                def rstd(src, srcname):
                    bk = pscount[0] % 2
                    pscount[0] += 1
                    pn = "ps%d" % bk
                    S.op("act", lambda e: e.activation(out=sq[:, :, 0:n], in_=src[:, :, 0:n], func=AF.Square), reads=[srcname], writes=["np_sq"])
                    for kc in range(16):
                        S.op("pe", lambda e, kc=kc: e.matmul(psf(bk)[:, 0:n], lhsT=ones_b[:], rhs=sq[:, kc, 0:n], start=(kc == 0), stop=(kc == 15)),
                             reads=["np_sq", "ones_b"], writes=[pn])
                    S.op("act", lambda e: e.activation(out=rs[:, 0:n], in_=psf(bk)[:, 0:n], func=AF.Sqrt, scale=1.0 / D, bias=epsT[:, 0:1]),
                         reads=[pn, "epsT"], writes=["np_rs"])
                    S.op("dve", lambda e: e.reciprocal(rs[:, 0:n], rs[:, 0:n]), reads=["np_rs"], writes=["np_rs"])

                def scale_by(dst, dstname, src, srcname, gidx, eng2="pool"):
                    S.op("dve", lambda e: e.tensor_tensor(tmp[:, :, 0:n], src[:, :, 0:n], rs[:, 0:n].unsqueeze(1).broadcast_to([128, 16, n]), ALU.mult),
                         reads=[srcname, "np_rs"], writes=["np_tmp"])
                    S.op(eng2, lambda e: e.tensor_tensor(dst, tmp[:, :, 0:n], gvec[:, gidx, l, :].unsqueeze(2).broadcast_to([128, 16, n]), ALU.mult),
                         reads=["np_tmp", "gvec"], writes=[dstname])

                if m_gain is not None:
                    rstd(mt[j], mb)
                    scale_by(mt[j][:, :, 0:n], mb, mt[j], mb, m_gain)
                    S.op("dve", lambda e: e.tensor_tensor(ht[j][:, :, 0:n], ht[j][:, :, 0:n], mt[j][:, :, 0:n], ALU.add), reads=[hb, mb], writes=[hb])
                    S.dma("sp", hT_v[:, :, t0:t0 + n], ht[j][:, :, 0:n], hb, reads=[hb], writes=["hT"])
                if pre_gain is not None:
                    rstd(ht[j], hb)
                    scale_by(XT[:, :, t0 - xt_t0:t0 - xt_t0 + n], xt_name, ht[j], hb, pre_gain)
            S.drain()
        S.forget(["np_ht0", "np_ht1", "np_mt0", "np_mt1", "np_sq", "np_tmp", "np_rs"])

    with ExitStack() as es:
        xin = [sb(es, "xin%d" % i, [128, D], F32) for i in range(2)]
        xo = [sb(es, "xo%d" % i, [128, 16, 128], F32) for i in range(2)]
        for b, (t0, n) in enumerate(BLOCKS):
            j = b % 2
            src = meta[:, :] if b == 0 else x[t0 - 16:t0 - 16 + n, :]
            S.dma("sp", xin[j][0:n, :], src, "xin%d" % j, writes=["xin%d" % j])
            for q4 in range(4):
                bk = pscount[0] % 2
                pscount[0] += 1
                for i4 in range(4):
                    kc = q4 * 4 + i4
                    S.op("pe", lambda e, kc=kc, i4=i4, bk=bk: e.transpose(psf(bk)[:, i4 * 128:i4 * 128 + n], xin[j][0:n, kc * 128:(kc + 1) * 128], ident_f[0:n, 0:n]),
                         reads=["xin%d" % j, "ident_f"], writes=["ps%d" % bk])
                S.op("act" if q4 % 2 else "dve",
                     (lambda e, bk=bk, q4=q4: e.activation(out=xo[j][:, q4 * 4:q4 * 4 + 4, 0:n], in_=psf(bk)[:].rearrange("p (a b) -> p a b", a=4)[:, :, 0:n], func=AF.Copy))
                     if q4 % 2 else
                     (lambda e, bk=bk, q4=q4: e.tensor_copy(xo[j][:, q4 * 4:q4 * 4 + 4, 0:n], psf(bk)[:].rearrange("p (a b) -> p a b", a=4)[:, :, 0:n])),
                     reads=["ps%d" % bk], writes=["xo%d" % j])
            S.dma("sp", hT_v[:, :, t0:t0 + n], xo[j][:, :, 0:n], "xo%d" % j, reads=["xo%d" % j], writes=["hT"])
        S.drain()
    S.forget(["xin0", "xin1", "xo0", "xo1"])

    def finish():
        S.drain()
        with nc.Block() as block:
            S.replay(block)
        es_all.close()
        return nc

    if stop_after == "init":
        return finish()

    for l in range(DEPTH):
        with ExitStack() as es:
            hnT = sb(es, "hnT", [128, 16, T], BF16)
            norm_pass(l, [(0, T)], None, 0, hnT, "hnT", 0, "p1")
            rot = sb(es, "rot2", [128, 4, 17, 128], F32)
            oml = sb(es, "oml2", [128, 1, 1024], F32)
            w8 = sb(es, "w82", [128, 1, 2048], F32)
            S.dma("sp", rot[:].rearrange("p a b c -> p (a b c)"), rot_d, "rot", reads=["rot_d"], writes=["rot"])
            S.dma("sp", oml[:, 0, :], oml_d[:, l * 1024:(l + 1) * 1024], "oml", reads=["oml_d"], writes=["oml"])
            S.dma("sp", w8[:, 0, :], w8_d[:, l * 2048:(l + 1) * 2048], "w8", reads=["w8_d"], writes=["w8"])
            wb = [sb(es, "wb%d" % i, [128, 16, 512], BF16) for i in range(2)]
            st = [sb(es, "st%d" % i, [128, 512], F32) for i in range(2)]
            st2 = [sb(es, "stb%d" % i, [128, 512], F32) for i in range(2)]
            ob = [sb(es, "ob%d" % i, [128, 512], BF16) for i in range(2)]
            of = [sb(es, "of%d" % i, [128, 512], F32) for i in range(2)]
            wv = w_in[l].rearrange("(kc p) n -> p kc n", p=128)
            NG = INC // 512
            S.dma("pool", wb[0][:], wv[:, :, 0:512], "wb0", writes=["wb0"])
            it = 0
            for g in range(NG):
                wj = g % 2
                if g + 1 < NG:
                    S.dma("pool", wb[1 - wj][:], wv[:, :, (g + 1) * 512:(g + 2) * 512], "wb%d" % (1 - wj), writes=["wb%d" % (1 - wj)])
                wn = "wb%d" % wj
                if g < 24:
                    for b, (t0, n) in enumerate(BLOCKS):
                        bk = pscount[0] % 2
                        pscount[0] += 1
                        pn = "ps%d" % bk
                        j = it % 2
                        it += 1
                        for kc in range(16):
                            S.op("pe", lambda e, kc=kc, bk=bk, wj=wj, t0=t0, n=n: e.matmul(psf(bk)[0:n, :], lhsT=hnT[:, kc, t0:t0 + n], rhs=wb[wj][:, kc, :],
                                                                                       start=(kc == 0), stop=(kc == 15)),
                                 reads=["hnT", wn], writes=[pn])
                        P = psf(bk)
                        stn, st2n, obn, ofn = "st%d" % j, "stb%d" % j, "ob%d" % j, "of%d" % j
                        if g < 4:
                            tb = 0 if g < 2 else 2
                            dst = (Qs["r"] if g < 2 else Ks["r"])[t0:t0 + n, (g % 2) * 512:(g % 2) * 512 + 512]
                            cc = rot[0:n, tb, b, :].unsqueeze(1).broadcast_to([n, 4, 128])
                            ss = rot[0:n, tb + 1, b, :].unsqueeze(1).broadcast_to([n, 4, 128])
                            P4 = P[0:n, :].rearrange("p (h c) -> p h c", h=4)
                            A4 = st[j][0:n, :].rearrange("p (h c) -> p h c", h=4)
                            B4 = st2[j][0:n, :].rearrange("p (h c) -> p h c", h=4)
                            O4 = ob[j][0:n, :].rearrange("p (h c) -> p h c", h=4)
                            S.op("dve", lambda e, A4=A4, P4=P4, cc=cc: e.tensor_tensor(A4, P4, cc, ALU.mult), reads=[pn, "rot"], writes=[stn])
                            S.op("dve", lambda e, B4=B4, P4=P4, ss=ss: e.tensor_tensor(B4, P4, ss, ALU.mult), reads=[pn, "rot"], writes=[st2n])
                            S.op("pool", lambda e, A4=A4, B4=B4, O4=O4: e.tensor_tensor(O4[:, :, 0:64], A4[:, :, 0:64], B4[:, :, 64:128], ALU.subtract),
                                 reads=[stn, st2n], writes=[obn])
                            S.op("pool", lambda e, A4=A4, B4=B4, O4=O4: e.tensor_tensor(O4[:, :, 64:128], B4[:, :, 0:64], A4[:, :, 64:128], ALU.add),
                                 reads=[stn, st2n], writes=[obn])
                            S.dma("sp", dst, ob[j][0:n, :], obn, reads=[obn], writes=["Q" if g < 2 else "K"])
                        elif g < 8 or 16 <= g < 20:
                            br = "r" if g < 8 else "h"
                            c0 = ((g - 4) if g < 8 else (g - 16)) * 512
                            S.op("act", lambda e, P=P, j=j, n=n: e.activation(out=ob[j][0:n, :], in_=P[0:n, :], func=AF.Copy), reads=[pn], writes=[obn])
                            S.dma("sp", Vs[br][t0:t0 + n, c0:c0 + 512], ob[j][0:n, :], obn, reads=[obn], writes=["V" + br])
                        elif g < 12:
                            c0 = (g - 8) * 512
                            S.op("act", lambda e, P=P, j=j, n=n: e.activation(out=ob[j][0:n, :], in_=P[0:n, :], func=AF.Silu), reads=[pn], writes=[obn])
                            S.dma("sp", Gs["r"][t0:t0 + n, c0:c0 + 512], ob[j][0:n, :], obn, reads=[obn], writes=["Gr"])
                        elif g < 14:
                            c0 = (g - 12) * 512
                            S.op("act", lambda e, P=P, j=j, n=n: e.activation(out=ob[j][0:n, :], in_=P[0:n, :], func=AF.Silu), reads=[pn], writes=[obn])
                            S.dma("sp", Qs["h"][t0:t0 + n, c0:c0 + 512], ob[j][0:n, :], obn, reads=[obn], writes=["Qh"])
                        elif g < 16:
                            c0 = (g - 14) * 512
                            S.op("act", lambda e, P=P, j=j, n=n: e.activation(out=st[j][0:n, :], in_=P[0:n, :], func=AF.Sigmoid, scale=-1.0), reads=[pn], writes=[stn])
                            S.op("dve", lambda e, j=j, n=n, c0=c0: e.tensor_tensor(st[j][0:n, :], st[j][0:n, :], oml[0:n, 0, c0:c0 + 512], ALU.mult),
                                 reads=[stn, "oml"], writes=[stn])
                            S.op("act", lambda e, j=j, n=n: e.activation(out=of[j][0:n, :], in_=st[j][0:n, :], func=AF.Ln, scale=-1.0, bias=oneT[0:n, 0:1]), reads=[stn, "oneT"], writes=[ofn])
                            S.op("pool", lambda e, j=j, n=n: e.tensor_copy(ob[j][0:n, :], st[j][0:n, :]), reads=[stn], writes=[obn])
                            S.dma("sp", Ks["h"][t0:t0 + n, c0:c0 + 512], ob[j][0:n, :], obn, reads=[obn], writes=["Kh"])
                            S.dma("sp", LFh[t0:t0 + n, c0:c0 + 512], of[j][0:n, :], ofn, reads=[ofn], writes=["LFh"])
                        else:
                            c0 = (g - 20) * 512
                            S.op("act", lambda e, P=P, j=j, n=n: e.activation(out=st[j][0:n, :], in_=P[0:n, :], func=AF.Silu), reads=[pn], writes=[stn])
                            S.op("dve", lambda e, j=j, n=n, c0=c0: e.tensor_tensor(ob[j][0:n, :], st[j][0:n, :], w8[0:n, 0, c0:c0 + 512], ALU.mult),
                                 reads=[stn, "w8"], writes=[obn])
                            S.dma("sp", Gs["h"][t0:t0 + n, c0:c0 + 512], ob[j][0:n, :], obn, reads=[obn], writes=["Gh"])
                else:
                    for cb in range(4):
                        row = (g - 24) * 4 + cb
                        for (t0, n) in tok_tiles(0, T):
                            bk = pscount[0] % 2
                            pscount[0] += 1
                            pn = "ps%d" % bk
                            j = it % 2
                            it += 1
                            for kc in range(16):
                                S.op("pe", lambda e, kc=kc, bk=bk, wj=wj, t0=t0, n=n, cb=cb: e.matmul(psf(bk)[:, 0:n], lhsT=wb[wj][:, kc, cb * 128:(cb + 1) * 128],
                                                                                                 rhs=hnT[:, kc, t0:t0 + n], start=(kc == 0), stop=(kc == 15)),
                                     reads=["hnT", wn], writes=[pn])
                            S.op("act", lambda e, bk=bk, j=j, n=n: e.activation(out=ob[j][:, 0:n], in_=psf(bk)[:, 0:n], func=AF.Sigmoid), reads=[pn], writes=["ob%d" % j])
                            S.dma("sp", GT_v[:, row, t0:t0 + n], ob[j][:, 0:n], "ob%d" % j, reads=["ob%d" % j], writes=["GT"])
            S.drain()
        S.forget(["hnT", "wb0", "wb1"] + ["%s%d" % (a, i) for a in ("st", "stb", "ob", "of") for i in range(2)])
        if stop_after == "inproj%d" % l:
            return finish()

        with ExitStack() as es:
            Sf = {br: sb(es, "Sf" + br, [128, 8, 256], F32) for br in "rh"}
            Sb = {br: sb(es, "Sb" + br, [128, 8, 256], BF16) for br in "rh"}
            for br in "rh":
                S.op("dve", lambda e, br=br: e.memset(Sf[br][:], 0.0), writes=["Sf" + br])
                S.op("pool", lambda e, br=br: e.memset(Sb[br][:], 0.0), writes=["Sb" + br])
            NB2 = 2
            Qt = {br: [sb(es, "Qt%s%d" % (br, i), [64, 1024], BF16) for i in range(NB2)] for br in "rh"}
            Kt = {br: [sb(es, "Kt%s%d" % (br, i), [64, 1024], BF16) for i in range(NB2)] for br in "rh"}
            Vt = {br: [sb(es, "Vt%s%d" % (br, i), [64, 2048], BF16) for i in range(NB2)] for br in "rh"}
            Gt = {br: [sb(es, "Gt%s%d" % (br, i), [64, 2048], BF16) for i in range(NB2)] for br in "rh"}
            Lt = [sb(es, "Lt%d" % i, [64, 1024], F32) for i in range(NB2)]
            E1 = sb(es, "E1", [128, 4, 64], F32)
            E2 = sb(es, "E2", [128, 4, 64], F32)
            EM = sb(es, "EM", [128, 4, 2], F32)
            E4 = sb(es, "E4", [64, 512], F32)
            QTt = sb(es, "QTt", [128, 4, 64], BF16)
            KTt = sb(es, "KTt", [128, 4, 64], BF16)
            QH = sb(es, "QH", [128, 4, 64], BF16)
            KE = sb(es, "KE", [64, 512], BF16)
            ATs = sb(es, "ATs", [64, 4, 64], BF16)
            SQ = sb(es, "SQ", [64, 4, 256], F32)
            SSm = sb(es, "SSm", [64, 4], F32)
            Yb = sb(es, "Yb", [64, 1024], BF16)
            YST = {br: [sb(es, "YST%s%d" % (br, i), [128, 16, 512], BF16) for i in range(2)] for br in "rh"}

            def load_sub(si):
                t0, n = SUBS[si]
                j = si % NB2
                for br in "rh":
                    S.dma("sp", Qt[br][j][0:n, :], Qs[br][t0:t0 + n, :], "Qt%s%d" % (br, j), reads=["Q" if br == "r" else "Qh"], writes=["Qt%s%d" % (br, j)])
                    S.dma("sp", Kt[br][j][0:n, :], Ks[br][t0:t0 + n, :], "Kt%s%d" % (br, j), reads=["K" if br == "r" else "Kh"], writes=["Kt%s%d" % (br, j)])
                    S.dma("sp", Vt[br][j][0:n, :], Vs[br][t0:t0 + n, :], "Vt%s%d" % (br, j), reads=["V" + br], writes=["Vt%s%d" % (br, j)])
                    S.dma("sp", Gt[br][j][0:n, :], Gs[br][t0:t0 + n, :], "Gt%s%d" % (br, j), reads=["G" + br], writes=["Gt%s%d" % (br, j)])
                S.dma("sp", Lt[j][0:n, :], LFh[t0:t0 + n, :], "Lt%d" % j, reads=["LFh"], writes=["Lt%d" % j])

            groups = tok_tiles(0, 16, 512) + tok_tiles(16, 2048, 512)
            grp_of = {}
            for gi, (g0, gn) in enumerate(groups):
                for si, (t0, n) in enumerate(SUBS):
                    if g0 <= t0 < g0 + gn:
                        grp_of[si] = gi
            EMp = [EM, sb(es, "EMb", [128, 4, 2], F32)]
            QHp = [QH, sb(es, "QHb", [128, 4, 64], BF16)]
            KEp = [KE, sb(es, "KEb", [64, 512], BF16)]
            ATp = [ATs, sb(es, "ATsb", [64, 4, 64], BF16)]
            steps = [(si, br, hh) for si in range(len(SUBS)) for br in "rh" for hh in range(2)]
            E1r = [sb(es, "E1r%d" % i, [128, 4, 64], F32) for i in range(2)]
            E2r = [sb(es, "E2r%d" % i, [128, 4, 64], F32) for i in range(2)]
            EMr = [sb(es, "EMr%d" % i, [128, 4, 2], F32) for i in range(2)]
            E4r = [sb(es, "E4r%d" % i, [64, 512], F32) for i in range(2)]

            class C:
                pass

            def ctx(k):
                c = C()
                c.si, c.br, c.hh = steps[k]
                c.t0, c.n = SUBS[c.si]
                c.j = c.si % NB2
                c.p = k % 2
                c.mm = mm16 if c.n == 16 else mm64
                c.mmn = "mm16" if c.n == 16 else "mm64"
                c.gi = grp_of[c.si]
                c.g0, c.gn = groups[c.gi]
                c.sj = c.gi % 2
                br, j = c.br, c.j
                c.LF = lfr if br == "r" else Lt[j]
                c.LFn = "lfr" if br == "r" else "Lt%d" % j
                c.Qn, c.Kn, c.Vn, c.Gn = ("Qt%s%d" % (br, j), "Kt%s%d" % (br, j), "Vt%s%d" % (br, j), "Gt%s%d" % (br, j))
                c.Q_, c.K_, c.V_, c.G_ = Qt[br][j], Kt[br][j], Vt[br][j], Gt[br][j]
                c.ystn = "YST%s%d" % (br, c.sj)
                c.c0 = c.hh * 512
                c.v0 = c.hh * 1024
                c.EM, c.QH, c.KE, c.AT = EMp[c.p], QHp[c.p], KEp[c.p], ATp[c.p]
                c.EMn, c.QHn, c.KEn, c.ATn = "EM%d" % c.p, "QH%d" % c.p, "KE%d" % c.p, "AT%d" % c.p
                c.E1, c.E2, c.E4 = E1, E2, E4
                c.E1n, c.E2n, c.E4n = "E1", "E2", "E4"
                c.comp = True
                if br == "r" and c.si >= 1:
                    hh = c.hh
                    c.E1, c.E2, c.E4, c.EM = E1r[hh], E2r[hh], E4r[hh], EMr[hh]
                    c.E1n, c.E2n, c.E4n, c.EMn = "E1r%d" % hh, "E2r%d" % hh, "E4r%d" % hh, "EMr%d" % hh
                    c.comp = (c.si == 1)
                return c

            PT = psb(2)
            PA = psf(2)[:, 256:512].rearrange("p (h c) -> p h c", h=4)
            PB = psf(3)[:, 0:264].rearrange("p (h c) -> p h c", h=4)
            PT5 = psb(5)

            def stage_a1(k):
                c = ctx(k)
                n, c0 = c.n, c.c0
                for h in range(4):
                    S.op("pe", lambda e: e.transpose(PT[:, h * 64:h * 64 + n], c.Q_[0:n, c0 + h * 128:c0 + (h + 1) * 128], ident_b[0:n, 0:n]),
                         reads=[c.Qn, "ident_b"], writes=["ps2"])
                    S.op("pe", lambda e: e.transpose(PT[:, 256 + h * 64:256 + h * 64 + n], c.K_[0:n, c0 + h * 128:c0 + (h + 1) * 128], ident_b[0:n, 0:n]),
                         reads=[c.Kn, "ident_b"], writes=["ps2"])
                if not c.comp:
                    return
                for h in range(4):
                    S.op("pe", lambda e: e.matmul(PB[:, h, 0:n + 2], lhsT=c.LF[0:n, c0 + h * 128:c0 + (h + 1) * 128], rhs=c.mm[0:n, 0:n + 2], start=True, stop=True),
                         reads=[c.LFn, c.mmn], writes=["ps3"])
                S.op("pe", lambda e: e.matmul(psf(4)[0:n, :], lhsT=m2[0:n, 0:n], rhs=c.LF[0:n, c0:c0 + 512], start=True, stop=True),
                     reads=[c.LFn, "m2"], writes=["ps4"])

            def stage_a2(k):
                c = ctx(k)
                n, c0 = c.n, c.c0
                E1, E2, E4 = c.E1, c.E2, c.E4
                if c.comp:
                    S.op("act", lambda e: e.activation(out=E1[:, :, 0:n], in_=PB[:, :, 0:n], func=AF.Exp), reads=["ps3"], writes=[c.E1n])
                    S.op("act", lambda e: e.activation(out=E2[:, :, 0:n], in_=PB[:, :, 0:n], func=AF.Exp, scale=-1.0), reads=["ps3"], writes=[c.E2n])
                    S.op("act", lambda e: e.activation(out=c.EM[:], in_=PB[:, :, n:n + 2], func=AF.Exp), reads=["ps3"], writes=[c.EMn])
                    S.op("act", lambda e: e.activation(out=E4[0:n, :], in_=psf(4)[0:n, :], func=AF.Exp), reads=["ps4"], writes=[c.E4n])
                PTq = PT[:, 0:256].rearrange("p (h c) -> p h c", h=4)[:, :, 0:n]
                PTk = PT[:, 256:512].rearrange("p (h c) -> p h c", h=4)[:, :, 0:n]
                S.op("dve", lambda e: e.tensor_tensor(QTt[:, :, 0:n], PTq, E1[:, :, 0:n], ALU.mult), reads=["ps2", c.E1n], writes=["QTt"])
                S.op("dve", lambda e: e.tensor_tensor(KTt[:, :, 0:n], PTk, E2[:, :, 0:n], ALU.mult), reads=["ps2", c.E2n], writes=["KTt"])
                S.op("pool", lambda e: e.tensor_tensor(c.QH[:, :, 0:n], QTt[:, :, 0:n], c.EM[:, :, 0:1].broadcast_to([128, 4, n]), ALU.mult),
                     reads=["QTt", c.EMn], writes=[c.QHn])
                S.op("pool", lambda e: e.tensor_tensor(c.KE[0:n, :], c.K_[0:n, c0:c0 + 512], E4[0:n, :], ALU.mult), reads=[c.Kn, c.E4n], writes=[c.KEn])
                for h in range(4):
                    S.op("pe", lambda e: e.matmul(PA[0:n, h, 0:n], lhsT=KTt[:, h, 0:n], rhs=QTt[:, h, 0:n], start=True, stop=True),
                         reads=["KTt", "QTt"], writes=["ps2"])

            def stage_a2b(k):
                c = ctx(k)
                n = c.n
                S.op("dve", lambda e: e.tensor_tensor(c.AT[0:n, :, 0:n], PA[0:n, :, 0:n], mask64[0:n, 0:n].unsqueeze(1).broadcast_to([n, 4, n]), ALU.mult),
                     reads=["ps2", "mask64"], writes=[c.ATn])

            def stage_b1(k):
                c = ctx(k)
                n, v0, br, hh = c.n, c.v0, c.br, c.hh
                for h in range(4):
                    po = psf(6 + h // 2)[0:n, (h % 2) * 256:(h % 2) * 256 + 256]
                    S.op("pe", lambda e: e.matmul(po, lhsT=c.AT[0:n, h, 0:n], rhs=c.V_[0:n, v0 + h * 256:v0 + (h + 1) * 256], start=True, stop=False),
                         reads=[c.ATn, c.Vn], writes=["ps%d" % (6 + h // 2)])
                    S.op("pe", lambda e: e.matmul(po, lhsT=c.QH[:, h, 0:n], rhs=Sb[br][:, hh * 4 + h, :], start=False, stop=True),
                         reads=[c.QHn, "Sb" + br], writes=["ps%d" % (6 + h // 2)])
                for h in range(4):
                    pd = psf(h // 2)[:, (h % 2) * 256:(h % 2) * 256 + 256]
                    S.op("pe", lambda e: e.matmul(pd, lhsT=c.KE[0:n, h * 128:(h + 1) * 128], rhs=c.V_[0:n, v0 + h * 256:v0 + (h + 1) * 256], start=True, stop=True),
                         reads=[c.KEn, c.Vn], writes=["ps%d" % (h // 2)])

            Ybp = [Yb, sb(es, "Ybb", [64, 1024], BF16)]
            RSm = sb(es, "RSm", [64, 4], F32)

            def stage_b2(k):
                c = ctx(k)
                n, v0, br, hh = c.n, c.v0, c.br, c.hh
                Ybk = Ybp[c.p]
                Ybn = "Yb%d" % c.p
                for k2 in range(2):
                    S.op("act", lambda e: e.activation(out=SQ[0:n, 2 * k2:2 * k2 + 2, :], in_=psf(6 + k2)[0:n, :].rearrange("p (a c) -> p a c", a=2), func=AF.Square),
                         reads=["ps%d" % (6 + k2)], writes=["SQ"])
                S.op("dve", lambda e: e.tensor_reduce(out=SSm[0:n, :], in_=SQ[0:n, :, :], axis=AX.X, op=ALU.add), reads=["SQ"], writes=["SSm"])
                S.op("act", lambda e: e.activation(out=RSm[0:n, :], in_=SSm[0:n, :], func=AF.Ln, scale=1.0 / 256, bias=epsT[0:n, 0:1]), reads=["SSm", "epsT"], writes=["RSm"])
                S.op("act", lambda e: e.activation(out=RSm[0:n, :], in_=RSm[0:n, :], func=AF.Exp, scale=-0.5), reads=["RSm"], writes=["RSm"])

            def stage_b2c(k):
                c = ctx(k)
                n, v0, br, hh = c.n, c.v0, c.br, c.hh
                Ybk = Ybp[c.p]
                Ybn = "Yb%d" % c.p
                for h in range(4):
                    S.op("dve", lambda e: e.scalar_tensor_tensor(out=Ybk[0:n, h * 256:(h + 1) * 256], in0=psf(6 + h // 2)[0:n, (h % 2) * 256:(h % 2) * 256 + 256],
                                                                 scalar=RSm[0:n, h:h + 1], in1=c.G_[0:n, v0 + h * 256:v0 + (h + 1) * 256], op0=ALU.mult, op1=ALU.mult),
                         reads=["ps%d" % (6 + h // 2), "RSm", c.Gn], writes=[Ybn])

            def stage_b2b(k):
                c = ctx(k)
                n, v0, br, hh = c.n, c.v0, c.br, c.hh
                for h in range(4):
                    S.op("dve", lambda e: e.scalar_tensor_tensor(out=Sf[br][:, hh * 4 + h, :], in0=Sf[br][:, hh * 4 + h, :], scalar=c.EM[:, h, 1:2],
                                                                 in1=psf(h // 2)[:, (h % 2) * 256:(h % 2) * 256 + 256], op0=ALU.mult, op1=ALU.add),
                         reads=["Sf" + br, c.EMn, "ps%d" % (h // 2)], writes=["Sf" + br])
                S.op("act", lambda e: e.activation(out=Sb[br][:, hh * 4:hh * 4 + 4, :], in_=Sf[br][:, hh * 4:hh * 4 + 4, :], func=AF.Copy),
                     reads=["Sf" + br], writes=["Sb" + br])

            def stage_c(k):
                c = ctx(k)
                n, br, hh, t0, g0, gn, sj = c.n, c.br, c.hh, c.t0, c.g0, c.gn, c.sj
                Ybk = Ybp[c.p]
                Ybn = "Yb%d" % c.p
                for c8 in range(8):
                    S.op("pe", lambda e: e.transpose(PT5[:, c8 * 64:c8 * 64 + n], Ybk[0:n, c8 * 128:(c8 + 1) * 128], ident_b[0:n, 0:n]),
                         reads=[Ybn, "ident_b"], writes=["ps5"])
                S.op("act", lambda e: e.activation(out=YST[br][sj][:, hh * 8:hh * 8 + 8, t0 - g0:t0 - g0 + n],
                                                   in_=PT5[:, 0:512].rearrange("p (a c) -> p a c", a=8)[:, :, 0:n], func=AF.Copy),
                     reads=["ps5"], writes=[c.ystn])
                if hh == 1 and t0 + n == g0 + gn:
                    S.dma("sp", yT_v[br][:, :, g0:g0 + gn], YST[br][sj][:, :, 0:gn], c.ystn, reads=[c.ystn], writes=["yT" + br])

            load_sub(0)
            load_sub(1)
            stage_a1(0)
            stage_a2(0)
            stage_a2b(0)
            NS = len(steps)
            for k in range(NS):
                if k + 1 < NS:
                    stage_a1(k + 1)
                stage_b1(k)
                if k + 1 < NS:
                    stage_a2(k + 1)
                stage_b2(k)
                stage_b2b(k)
                if k + 1 < NS:
                    stage_a2b(k + 1)
                stage_b2c(k)
                if k >= 1:
                    stage_c(k - 1)
                    si_, br_, hh_ = steps[k - 1]
                    if (br_, hh_) == ("h", 1) and si_ + 2 < len(SUBS):
                        load_sub(si_ + 2)
            stage_c(NS - 1)
            S.drain()
        S.forget([k for k in list(S.last_w.keys()) if isinstance(k, str) and (k[:2] in ("Qt", "Kt", "Vt", "Gt", "Lt", "YS", "Sf", "Sb") or
                                                                             k in ("E1", "E2", "EM0", "EM1", "E4", "QTt", "KTt", "QH0", "QH1", "KE0", "KE1", "AT0", "AT1", "TMPS", "SQ", "SSm", "Yf", "Yb0", "Yb1", "RSm"))])
        if stop_after == "mixer%d" % l:
            return finish()

        for hi_, (h0, hn_) in enumerate(HALVES):
            tiles = tok_tiles(h0, hn_)
            with ExitStack() as es:
                with ExitStack() as es2:
                    yTs = sb(es2, "yTs", [128, 16, 1040], BF16)
                    with ExitStack() as es3:
                        yin = {br: sb(es3, "yin" + br, [128, 16, 1040], BF16) for br in "rh"}
                        for br in "rh":
                            S.dma("sp", yin[br][:, :, 0:hn_], yT_v[br][:, :, h0:h0 + hn_], "yin" + br, reads=["yT" + br], writes=["yin" + br])
                        wr = {br: [sb(es3, "wr%s%d" % (br, i), [128, 16, 256], BF16) for i in range(2)] for br in "rh"}
                        gt = {br: [sb(es3, "gt%s%d" % (br, i), [128, 512], BF16) for i in range(2)] for br in "rh"}
                        t1 = [sb(es3, "t1_%d" % i, [128, 512], F32) for i in range(2)]
                        t2 = [sb(es3, "t2_%d" % i, [128, 512], F32) for i in range(2)]
                        wsrc = {"r": w_br_ret[l].rearrange("(kc p) n -> p kc n", p=128), "h": w_br_hg[l].rearrange("(kc p) n -> p kc n", p=128)}
                        for br in "rh":
                            S.dma("pool", wr[br][0][:], wsrc[br][:, :, 0:256], "wr%s0" % br, writes=["wr%s0" % br])
                        it = 0
                        for g in range(8):
                            wj = g % 2
                            if g + 1 < 8:
                                for br in "rh":
                                    S.dma("pool", wr[br][1 - wj][:], wsrc[br][:, :, (g + 1) * 256:(g + 2) * 256], "wr%s%d" % (br, 1 - wj), writes=["wr%s%d" % (br, 1 - wj)])
                            for cb in range(2):
                                row = g * 2 + cb
                                for (t0, n) in tiles:
                                    j = it % 2
                                    it += 1
                                    bks = {"r": 2 * j, "h": 2 * j + 1}
                                    for bi, br in enumerate("rh"):
                                        S.dma("sp", gt[br][j][:, 0:n], GT_v[:, bi * 16 + row, t0:t0 + n], "gt%s%d" % (br, j), reads=["GT"], writes=["gt%s%d" % (br, j)])
                                        bk = bks[br]
                                        for kc in range(16):
                                            S.op("pe", lambda e, kc=kc, bk=bk, br=br, wj=wj, cb=cb, t0=t0, n=n: e.matmul(
                                                psf(bk)[:, 0:n], lhsT=wr[br][wj][:, kc, cb * 128:(cb + 1) * 128], rhs=yin[br][:, kc, t0 - h0:t0 - h0 + n],
                                                start=(kc == 0), stop=(kc == 15)), reads=["yin" + br, "wr%s%d" % (br, wj)], writes=["ps%d" % bk])
                                    S.op("dve", lambda e, j=j, n=n, bk=bks["r"]: e.tensor_tensor(t1[j][:, 0:n], psf(bk)[:, 0:n], gt["r"][j][:, 0:n], ALU.mult),
                                         reads=["ps%d" % bks["r"], "gtr%d" % j], writes=["t1_%d" % j])
                                    S.op("dve", lambda e, j=j, n=n, bk=bks["h"]: e.tensor_tensor(t2[j][:, 0:n], psf(bk)[:, 0:n], gt["h"][j][:, 0:n], ALU.mult),
                                         reads=["ps%d" % bks["h"], "gth%d" % j], writes=["t2_%d" % j])
                                    S.op("pool", lambda e, j=j, n=n, row=row, t0=t0: e.tensor_tensor(yTs[:, row, t0 - h0:t0 - h0 + n], t1[j][:, 0:n], t2[j][:, 0:n], ALU.add),
                                         reads=["t1_%d" % j, "t2_%d" % j], writes=["yTs"])
                        S.drain()
                    S.forget(["yinr", "yinh"] + ["%s%s%d" % (a, br, i) for a in ("wr", "gt") for br in "rh" for i in range(2)] + ["t1_0", "t1_1", "t2_0", "t2_1"])
                    with ExitStack() as es3:
                        wo = [sb(es3, "wo%d" % i, [128, 16, 512], BF16) for i in range(2)]
                        mo = [sb(es3, "mo%d" % i, [128, 512], F32) for i in range(2)]
                        wsrc = w_out[l].rearrange("(kc p) n -> p kc n", p=128)
                        S.dma("pool", wo[0][:], wsrc[:, :, 0:512], "wo0", writes=["wo0"])
                        it = 0
                        for g in range(4):
                            wj = g % 2
                            if g + 1 < 4:
                                S.dma("pool", wo[1 - wj][:], wsrc[:, :, (g + 1) * 512:(g + 2) * 512], "wo%d" % (1 - wj), writes=["wo%d" % (1 - wj)])
                            for cb in range(4):
                                row = g * 4 + cb
                                for (t0, n) in tiles:
                                    j = it % 2
                                    it += 1
                                    bk = j
                                    for kc in range(16):
                                        S.op("pe", lambda e, kc=kc, bk=bk, wj=wj, cb=cb, t0=t0, n=n: e.matmul(
                                            psf(bk)[:, 0:n], lhsT=wo[wj][:, kc, cb * 128:(cb + 1) * 128], rhs=yTs[:, kc, t0 - h0:t0 - h0 + n],
                                            start=(kc == 0), stop=(kc == 15)), reads=["yTs", "wo%d" % wj], writes=["ps%d" % bk])
                                    S.op("act", lambda e, j=j, n=n, bk=bk: e.activation(out=mo[j][:, 0:n], in_=psf(bk)[:, 0:n], func=AF.Copy), reads=["ps%d" % bk], writes=["mo%d" % j])
                                    S.dma("sp", mT_v[:, row, t0:t0 + n], mo[j][:, 0:n], "mo%d" % j, reads=["mo%d" % j], writes=["mT"])
                        S.drain()
                    S.forget(["wo0", "wo1", "mo0", "mo1", "yTs"])
                if stop_after == "p4_%d_%d" % (l, hi_):
                    return finish()
                hn2T = sb(es, "hn2T", [128, 16, 1040], BF16)
                norm_pass(l, [(h0, hn_)], 1, 2, hn2T, "hn2T", h0, "p5")
                if stop_after == "p5_%d_%d" % (l, hi_):
                    return finish()
                with ExitStack() as es2:
                    hidT = sb(es2, "hidT", [128, KCF, 1040], BF16)
                    with ExitStack() as es3:
                        wg = {m_: [sb(es3, "wg%s%d" % (m_, i), [128, 16, 256], BF16) for i in range(2)] for m_ in "gu"}
                        sg = [sb(es3, "sg%d" % i, [128, 512], F32) for i in range(2)]
                        wsrc = {"g": w_gate[l].rearrange("(kc p) n -> p kc n", p=128), "u": w_up[l].rearrange("(kc p) n -> p kc n", p=128)}
                        NGF = DFF // 256
                        for m_ in "gu":
                            S.dma("pool", wg[m_][0][:], wsrc[m_][:, :, 0:256], "wg%s0" % m_, writes=["wg%s0" % m_])
                        it = 0
                        for g in range(NGF):
                            wj = g % 2
                            if g + 1 < NGF:
                                for m_ in "gu":
                                    S.dma("pool", wg[m_][1 - wj][:], wsrc[m_][:, :, (g + 1) * 256:(g + 2) * 256], "wg%s%d" % (m_, 1 - wj), writes=["wg%s%d" % (m_, 1 - wj)])
                            for cb in range(2):
                                fc = g * 2 + cb
                                for (t0, n) in tiles:
                                    j = it % 2
                                    it += 1
                                    bks = {"g": 2 * j, "u": 2 * j + 1}
                                    for m_ in "gu":
                                        bk = bks[m_]
                                        for kc in range(16):
                                            S.op("pe", lambda e, kc=kc, bk=bk, m_=m_, wj=wj, cb=cb, t0=t0, n=n: e.matmul(
                                                psf(bk)[:, 0:n], lhsT=wg[m_][wj][:, kc, cb * 128:(cb + 1) * 128], rhs=hn2T[:, kc, t0 - h0:t0 - h0 + n],
                                                start=(kc == 0), stop=(kc == 15)), reads=["hn2T", "wg%s%d" % (m_, wj)], writes=["ps%d" % bk])
                                    S.op("act", lambda e, j=j, n=n, bk=bks["g"]: e.activation(out=sg[j][:, 0:n], in_=psf(bk)[:, 0:n], func=AF.Silu),
                                         reads=["ps%d" % bks["g"]], writes=["sg%d" % j])
                                    S.op("dve", lambda e, j=j, n=n, bk=bks["u"], fc=fc, t0=t0: e.tensor_tensor(hidT[:, fc, t0 - h0:t0 - h0 + n], psf(bk)[:, 0:n], sg[j][:, 0:n], ALU.mult),
                                         reads=["ps%d" % bks["u"], "sg%d" % j], writes=["hidT"])
                        S.drain()
                    S.forget(["wgg0", "wgg1", "wgu0", "wgu1", "sg0", "sg1"])
                    with ExitStack() as es3:
                        wd = [sb(es3, "wd%d" % i, [128, KCF, 128], BF16) for i in range(2)]
                        mo = [sb(es3, "fo%d" % i, [128, 512], F32) for i in range(2)]
                        wsrc = w_down[l].rearrange("(kc p) n -> p kc n", p=128)
                        S.dma("pool", wd[0][:], wsrc[:, :, 0:128], "wd0", writes=["wd0"])
                        it = 0
                        for cbk in range(16):
                            wj = cbk % 2
                            if cbk + 1 < 16:
                                S.dma("pool", wd[1 - wj][:], wsrc[:, :, (cbk + 1) * 128:(cbk + 2) * 128], "wd%d" % (1 - wj), writes=["wd%d" % (1 - wj)])
                            for (t0, n) in tiles:
                                j = it % 2
                                it += 1
                                bk = j
                                for kc in range(KCF):
                                    S.op("pe", lambda e, kc=kc, bk=bk, wj=wj, t0=t0, n=n: e.matmul(
                                        psf(bk)[:, 0:n], lhsT=wd[wj][:, kc, :], rhs=hidT[:, kc, t0 - h0:t0 - h0 + n],
                                        start=(kc == 0), stop=(kc == KCF - 1)), reads=["hidT", "wd%d" % wj], writes=["ps%d" % bk])
                                S.op("act", lambda e, j=j, n=n, bk=bk: e.activation(out=mo[j][:, 0:n], in_=psf(bk)[:, 0:n], func=AF.Copy), reads=["ps%d" % bk], writes=["fo%d" % j])
                                S.dma("sp", mT_v[:, cbk, t0:t0 + n], mo[j][:, 0:n], "fo%d" % j, reads=["fo%d" % j], writes=["mT"])
                        S.drain()
                    S.forget(["wd0", "wd1", "fo0", "fo1", "hidT"])
                S.forget(["hn2T"])
            norm_pass(l, [(h0, hn_)], 3, None, None, None, 0, "p7")
        if stop_after == "layer%d" % l:
            return finish()

    with ExitStack() as es:
        hi2 = [sb(es, "hi2_%d" % i, [128, 16, 128], F32) for i in range(2)]
        oo = [sb(es, "oo%d" % i, [128, D], F32) for i in range(2)]
        for b, (t0, n) in enumerate(BLOCKS):
            if b == 0:
                continue
            j = b % 2
            S.dma("sp", hi2[j][:], hT_v[:, :, t0:t0 + n], "hi2_%d" % j, reads=["hT"], writes=["hi2_%d" % j])
            for q4 in range(4):
                bk = pscount[0] % 2
                pscount[0] += 1
                for i4 in range(4):
                    kc = q4 * 4 + i4
                    S.op("pe", lambda e, kc=kc, i4=i4, bk=bk, j=j: e.transpose(psf(bk)[:, i4 * 128:(i4 + 1) * 128], hi2[j][:, kc, :], ident_f[:]),
                         reads=["hi2_%d" % j, "ident_f"], writes=["ps%d" % bk])
                if q4 % 2:
                    S.op("act", lambda e, bk=bk, q4=q4, j=j: e.activation(out=oo[j][:, q4 * 512:(q4 + 1) * 512], in_=psf(bk)[:], func=AF.Copy), reads=["ps%d" % bk], writes=["oo%d" % j])
                else:
                    S.op("dve", lambda e, bk=bk, q4=q4, j=j: e.tensor_copy(oo[j][:, q4 * 512:(q4 + 1) * 512], psf(bk)[:]), reads=["ps%d" % bk], writes=["oo%d" % j])
            S.dma("sp", out[t0 - 16:t0 - 16 + n, :], oo[j][:], "oo%d" % j, reads=["oo%d" % j], writes=["out"])
    return finish()


_NC_CACHE = {}


def kernel(**inputs):
    names = ["meta_tokens", "norm_mix_pre", "norm_mix_post", "norm_ffn_pre", "norm_ffn_post", "w_in", "hg_lb_logits",
             "hg_norm_w", "w_br_ret", "w_br_hg", "w_out", "w_ffn_gate", "w_ffn_up", "w_ffn_down"]
    x = np.ascontiguousarray(np.asarray(inputs["x"], dtype=np.float32))
    shared = {n: np.ascontiguousarray(np.asarray(inputs[n], dtype=np.float32)) for n in names}
    if "nc" not in _NC_CACHE:
        _NC_CACHE["nc"] = build_nc()
    nc = _NC_CACHE["nc"]
    in_maps = []
    for c in range(8):
        m = dict(shared)
        m["x"] = x[c]
        in_maps.append(m)
    res = run_bass_kernel_spmd(nc, in_maps, core_ids=list(range(8)))
    return np.stack([np.asarray(r["out"], dtype=np.float32) for r in res.results], axis=0)
```
